# Optimizing a Trainium2 kernel written in Bass

```python
import jax
import jax.numpy as jnp
from jax import lax
import numpy as np

D_MODEL = 1024
BATCH = 16
SEQ = 256
DEPTH = 1
DEC_BATCH = 8
DEC_SEQ = 1024
PAST_LEN = 256

GRID_W = 64
HEAD_DIM = 64
D_RWKV = D_MODEL // 2
H_RWKV = D_RWKV // HEAD_DIM
D_ATTN = D_MODEL - D_RWKV
H_Q = D_ATTN // HEAD_DIM
H_KV = 2
GQA_GROUP = H_Q // H_KV
D_KV = H_KV * HEAD_DIM
LORA_W = 64
LORA_A = 64
LORA_G = 128
N_DIR = 2
D_FF = 2816
CONV_W = 3
Q_BLOCK = 128
ROPE_THETA = 10000.0
RMS_EPS = 1e-6
GN_EPS = 64e-5
IN_WIDTHS = (D_RWKV, D_RWKV, D_RWKV, LORA_W, LORA_A, LORA_G, D_ATTN, D_KV, D_KV)
D_IN = 3 * D_RWKV + LORA_W + LORA_A + LORA_G + D_ATTN + 2 * D_KV

kernel_name = 'hybrid_rwkv7_gqa_dit_step'


def rms_norm(x, g):
    xf = x.astype(jnp.float32)
    y = xf * lax.rsqrt(jnp.mean(xf * xf, axis=-1, keepdims=True) + RMS_EPS)
    return (y * g.astype(jnp.float32)).astype(x.dtype)


def ada_ln(cvec, w_mod, b_mod):
    m = jax.nn.silu(cvec) @ w_mod + b_mod
    return [t[:, None, :] for t in jnp.split(m, 6, axis=-1)]


def modulate(h, shift, scale):
    return h * (1.0 + scale) + shift


def heads(t):
    return t.reshape(t.shape[0], t.shape[1], -1, HEAD_DIM)


def split_projection(h, w_in):
    bounds = [int(b) for b in np.cumsum(IN_WIDTHS)[:-1]]
    return jnp.split(h @ w_in, bounds, axis=-1)


def grid_rope(x):
    n_tok = x.shape[1]
    n_rows = n_tok // GRID_W
    row = jnp.repeat(jnp.arange(n_rows), GRID_W).astype(jnp.float32)
    col = jnp.tile(jnp.arange(GRID_W), n_rows).astype(jnp.float32)
    half = HEAD_DIM // 2
    quarter = half // 2
    inv_freq = ROPE_THETA ** (-jnp.arange(quarter, dtype=jnp.float32) / quarter)

    def rotate(xp, pos):
        ang = pos[:, None] * inv_freq[None, :]
        cos = jnp.cos(ang)[None, :, None, :]
        sin = jnp.sin(ang)[None, :, None, :]
        x1, x2 = xp[..., :quarter], xp[..., quarter:]
        return jnp.concatenate([x1 * cos - x2 * sin, x1 * sin + x2 * cos], axis=-1)

    xf = x.astype(jnp.float32)
    out = jnp.concatenate([rotate(xf[..., :half], row), rotate(xf[..., half:], col)], axis=-1)
    return out.astype(x.dtype)


def block_attention(q, k, v):
    b, lq = q.shape[0], q.shape[1]
    n_blk = lq // Q_BLOCK
    qb = q.reshape(b, n_blk, Q_BLOCK, H_KV, GQA_GROUP, HEAD_DIM).transpose(1, 0, 2, 3, 4, 5)
    kf = k.astype(jnp.float32)
    vf = v.astype(jnp.float32)
    scale = HEAD_DIM ** -0.5

    def one_block(q_blk):
        s = jnp.einsum('bqkgd,bskd->bkgqs', q_blk.astype(jnp.float32), kf) * scale
        p = jax.nn.softmax(s, axis=-1)
        return jnp.einsum('bkgqs,bskd->bqkgd', p, vf).astype(q.dtype)

    o = lax.map(one_block, qb)
    return o.transpose(1, 0, 2, 3, 4, 5).reshape(b, lq, H_Q * HEAD_DIM)


def wkv_scan(r, w, k, v, kk, a, s0, reverse):
    def step(s, inp):
        r_t, w_t, k_t, v_t, kk_t, a_t = inp
        sa = jnp.einsum('bhvk,bhk->bhv', s, -kk_t)
        s = (s * w_t[:, :, None, :]
             + sa[..., None] * (kk_t * a_t)[:, :, None, :]
             + v_t[..., None] * k_t[:, :, None, :])
        return s, jnp.einsum('bhvk,bhk->bhv', s, r_t)

    xs = tuple(jnp.moveaxis(t, 1, 0) for t in (r, w, k, v, kk, a))
    s_fin, ys = lax.scan(step, s0, xs, reverse=reverse)
    return jnp.moveaxis(ys, 0, 1), s_fin


def rwkv_mixer(r, k, v, xw, xa, xg, s0, lp):
    f32 = jnp.float32
    out_dtype = r.dtype
    r, k, v, xw, xa, xg = (t.astype(f32) for t in (r, k, v, xw, xa, xg))
    g = jax.nn.sigmoid(xg) @ lp['g_up'].astype(f32)
    kk = heads(k * lp['k_k'].astype(f32))
    kk = kk * lax.rsqrt(jnp.sum(kk * kk, axis=-1, keepdims=True) + 1e-12)
    tw = jnp.tanh(xw)
    rh, vh = heads(r), heads(v)
    r_k = lp['r_k'].astype(f32)
    y_sum = jnp.zeros_like(rh)
    bonus = jnp.zeros_like(rh)
    finals = []
    for d in range(N_DIR):
        logw = -jax.nn.softplus(-(lp['w0'][d].astype(f32) + tw @ lp['w_up'][d].astype(f32))) - 0.5
        decay = jnp.exp(-jnp.exp(logw))
        a = jax.nn.sigmoid(lp['a0'][d].astype(f32) + xa @ lp['a_up'][d].astype(f32))
        kd = heads(k * (1.0 + (a - 1.0) * lp['k_a'].astype(f32)))
        y_d, s_d = wkv_scan(rh, heads(decay), kd, vh, kk, heads(a),
                            s0[:, d].astype(f32), reverse=(d == 1))
        y_sum = y_sum + y_d
        bonus = bonus + jnp.sum(rh * kd * r_k, axis=-1, keepdims=True) * vh
        finals.append(s_d)
    mu = jnp.mean(y_sum, axis=-1, keepdims=True)
    var = jnp.mean(jnp.square(y_sum - mu), axis=-1, keepdims=True)
    yn = ((y_sum - mu) * lax.rsqrt(var + GN_EPS)).reshape(r.shape)
    yn = yn * lp['gn_w'].astype(f32) + lp['gn_b'].astype(f32)
    y = (yn + bonus.reshape(r.shape)) * g
    return y.astype(out_dtype), jnp.stack(finals, axis=1)


def conv_ffn(h, lp):
    u = h @ lp['ffn_up']
    n_tok = u.shape[1]
    pad = CONV_W // 2
    up = jnp.pad(u, ((0, 0), (pad, pad), (0, 0)))
    conv = lp['conv_b'] + up[:, 0:n_tok] * lp['conv_w'][0]
    for j in range(1, CONV_W):
        conv = conv + up[:, j:j + n_tok] * lp['conv_w'][j]
    gate, val = jnp.split(conv, 2, axis=-1)
    return (jax.nn.silu(gate) * val) @ lp['ffn_down']


def trunk_layer(x, mod, lp, ctx=None):
    shift1, scale1, gate1, shift2, scale2, gate2 = mod
    b = x.shape[0]
    h = modulate(rms_norm(x, lp['norm_mix_pre']), shift1, scale1)
    r, k, v, xw, xa, xg, q, ka, va = split_projection(h, lp['w_in'])
    q = rms_norm(heads(q), lp['q_norm'])
    ka = rms_norm(heads(ka), lp['k_norm'])
    va = heads(va)
    if ctx is None:
        s0 = jnp.zeros((b, N_DIR, H_RWKV, HEAD_DIM, HEAD_DIM), jnp.float32)
        y_attn = block_attention(q, ka, va)
    else:
        k_ctx, v_ctx, s0 = ctx
        k_all = jnp.concatenate([grid_rope(ka), k_ctx.astype(ka.dtype)], axis=1)
        v_all = jnp.concatenate([va, v_ctx.astype(va.dtype)], axis=1)
        y_attn = block_attention(grid_rope(q), k_all, v_all)
    y_rwkv, s_fin = rwkv_mixer(r, k, v, xw, xa, xg, s0, lp)
    mix = jnp.concatenate([y_rwkv, y_attn], axis=-1) @ lp['w_out']
    x = x + gate1 * rms_norm(mix, lp['norm_mix_post'])
    h2 = modulate(rms_norm(x, lp['norm_ffn_pre']), shift2, scale2)
    x = x + gate2 * rms_norm(conv_ffn(h2, lp), lp['norm_ffn_post'])
    return x, ka, va, s_fin


def setup_inputs(seed: int = 0) -> dict:
    key = jax.random.key(seed)
    ks = jax.random.split(key, 32)

    def nrm(k, shape, scale):
        return scale * jax.random.normal(k, shape, jnp.float32)

    return {
        'x_prompt': nrm(ks[0], (BATCH, SEQ, D_MODEL), 1.0),
        'x_sample': nrm(ks[1], (DEC_BATCH, DEC_SEQ, D_MODEL), 1.0),
        'cache_k': nrm(ks[2], (DEC_BATCH, DEPTH, PAST_LEN, H_KV, HEAD_DIM), 1.0),
        'cache_v': nrm(ks[3], (DEC_BATCH, DEPTH, PAST_LEN, H_KV, HEAD_DIM), 0.5),
        'state_rwkv': nrm(ks[4], (DEC_BATCH, DEPTH, N_DIR, H_RWKV, HEAD_DIM, HEAD_DIM), 0.3),
        'c': nrm(ks[5], (DEC_BATCH, D_MODEL), 1.0),
        'c_ctx': nrm(ks[6], (D_MODEL,), 1.0),
        'w_mod': nrm(ks[7], (DEPTH, D_MODEL, 6 * D_MODEL), 0.5 * D_MODEL ** -0.5),
        'b_mod': nrm(ks[8], (DEPTH, 6 * D_MODEL), 0.1),
        'norm_mix_pre': 1.0 + nrm(ks[9], (DEPTH, D_MODEL), 0.05),
        'norm_mix_post': 1.0 + nrm(ks[10], (DEPTH, D_MODEL), 0.05),
        'norm_ffn_pre': 1.0 + nrm(ks[11], (DEPTH, D_MODEL), 0.05),
        'norm_ffn_post': 1.0 + nrm(ks[12], (DEPTH, D_MODEL), 0.05),
        'w_in': nrm(ks[13], (DEPTH, D_MODEL, D_IN), D_MODEL ** -0.5),
        'w0': 0.5 + nrm(ks[14], (DEPTH, N_DIR, D_RWKV), 0.5),
        'w_up': nrm(ks[15], (DEPTH, N_DIR, LORA_W, D_RWKV), 0.3 * LORA_W ** -0.5),
        'a0': nrm(ks[16], (DEPTH, N_DIR, D_RWKV), 0.3),
        'a_up': nrm(ks[17], (DEPTH, N_DIR, LORA_A, D_RWKV), 0.3 * LORA_A ** -0.5),
        'g_up': nrm(ks[18], (DEPTH, LORA_G, D_RWKV), LORA_G ** -0.5),
        'k_k': 0.85 + nrm(ks[19], (DEPTH, D_RWKV), 0.05),
        'k_a': 1.0 + nrm(ks[20], (DEPTH, D_RWKV), 0.05),
        'r_k': nrm(ks[21], (DEPTH, H_RWKV, HEAD_DIM), 0.1),
        'gn_w': 1.0 + nrm(ks[22], (DEPTH, D_RWKV), 0.05),
        'gn_b': nrm(ks[23], (DEPTH, D_RWKV), 0.02),
        'q_norm': 1.0 + nrm(ks[24], (DEPTH, HEAD_DIM), 0.05),
        'k_norm': 1.0 + nrm(ks[25], (DEPTH, HEAD_DIM), 0.05),
        'w_out': nrm(ks[26], (DEPTH, D_MODEL, D_MODEL), D_MODEL ** -0.5),
        'ffn_up': nrm(ks[27], (DEPTH, D_MODEL, 2 * D_FF), D_MODEL ** -0.5),
        'conv_w': nrm(ks[28], (DEPTH, CONV_W, 2 * D_FF), CONV_W ** -0.5),
        'conv_b': nrm(ks[29], (DEPTH, 2 * D_FF), 0.02),
        'ffn_down': nrm(ks[30], (DEPTH, D_FF, D_MODEL), D_FF ** -0.5),
    }


def reference(x_prompt, x_sample, cache_k, cache_v, state_rwkv, c, c_ctx,
              w_mod, b_mod, norm_mix_pre, norm_mix_post, norm_ffn_pre, norm_ffn_post,
              w_in, w0, w_up, a0, a_up, g_up, k_k, k_a, r_k, gn_w, gn_b,
              q_norm, k_norm, w_out, ffn_up, conv_w, conv_b, ffn_down):
    y_prompt = x_prompt
    y_sample = x_sample
    new_k, new_v, new_s = [], [], []
    for layer in range(DEPTH):
        lp = {
            'norm_mix_pre': norm_mix_pre[layer], 'norm_mix_post': norm_mix_post[layer],
            'norm_ffn_pre': norm_ffn_pre[layer], 'norm_ffn_post': norm_ffn_post[layer],
            'w_in': w_in[layer], 'w0': w0[layer], 'w_up': w_up[layer],
            'a0': a0[layer], 'a_up': a_up[layer], 'g_up': g_up[layer],
            'k_k': k_k[layer], 'k_a': k_a[layer], 'r_k': r_k[layer],
            'gn_w': gn_w[layer], 'gn_b': gn_b[layer],
            'q_norm': q_norm[layer], 'k_norm': k_norm[layer], 'w_out': w_out[layer],
            'ffn_up': ffn_up[layer], 'conv_w': conv_w[layer], 'conv_b': conv_b[layer],
            'ffn_down': ffn_down[layer],
        }
        mod_ctx = ada_ln(c_ctx[None, :], w_mod[layer], b_mod[layer])
        mod_lat = ada_ln(c, w_mod[layer], b_mod[layer])
        y_prompt, k_l, v_l, s_l = trunk_layer(y_prompt, mod_ctx, lp)
        new_k.append(k_l)
        new_v.append(v_l)
        new_s.append(s_l.astype(x_prompt.dtype))
        y_sample, _, _, _ = trunk_layer(
            y_sample, mod_lat, lp,
            ctx=(cache_k[:, layer], cache_v[:, layer], state_rwkv[:, layer]))
    new_cache_k = jnp.stack(new_k, axis=1)
    new_cache_v = jnp.stack(new_v, axis=1)
    new_state_rwkv = jnp.stack(new_s, axis=1)
    return (y_prompt, y_sample, new_cache_k, new_cache_v, new_state_rwkv)
```

```python
import numpy as np
from contextlib import ExitStack
import concourse.bass as bass
import concourse.mybir as mybir
from concourse.bass_utils import run_bass_kernel_spmd

F32 = mybir.dt.float32
BF16 = mybir.dt.bfloat16
AF = mybir.ActivationFunctionType
ALU = mybir.AluOpType
AX = mybir.AxisListType

NT = 12
NTOK = 1536
D = 1024
DFF = 2816
CDEC = -float(np.exp(-0.5))
SEQS = [(0, 8), (8, 2), (10, 2)]
KOFF = [0, 1280, 1536]
VT0 = [0, 10, 12]
NKT = [10, 2, 2]


class Prog:
    ENGS = ('pe', 'act', 'dve', 'pool', 'sp')

    def __init__(self, nc, self_sync=True):
        self.nc = nc
        self.ins = {e: [] for e in self.ENGS}
        self.lastw = {}
        self.readers = {}
        self.group_n = []
        self.group_pool = []
        self.self_sync = self_sync
        self.bar = set()
        self.capture = None
        self.closed = set()
        self.ps_last = {}
        self.bar_id = 0
        self.absorbed = {e: 0 for e in self.ENGS}

    def dma_group(self, pool=None):
        self.group_n.append(0)
        self.group_pool.append(pool)
        return len(self.group_n) - 1

    def _collect(self, reads, writes, eng=None):
        deps = set()
        for k in list(reads) + list(writes):
            if k.startswith('ps') and k[2:3].isdigit():
                for e2, tok in self.ps_last.get(k, {}).items():
                    if e2 != eng:
                        deps.add(tok)
        for k in reads:
            if k in self.lastw:
                deps.add(self.lastw[k])
        for k in writes:
            if k in self.lastw:
                deps.add(self.lastw[k])
            deps.update(self.readers.get(k, ()))
        return deps

    def _record(self, tok, reads, writes):
        if tok[0] == 'e':
            for k in list(reads) + list(writes):
                if k.startswith('ps') and k[2:3].isdigit():
                    self.ps_last.setdefault(k, {})[tok[1]] = tok
        for k in reads:
            self.readers.setdefault(k, []).append(tok)
        for k in writes:
            self.lastw[k] = tok
            self.readers[k] = []

    def op(self, eng, fn, reads=(), writes=()):
        if self.capture is not None:
            self.capture.append(('op', (eng, fn, tuple(reads), tuple(writes))))
            return
        deps = self._collect(reads, writes, eng) | self._bar_deps(eng)
        for d in deps:
            if d[0] == 'd':
                self.closed.add(d[1])
        idx = len(self.ins[eng])
        self.ins[eng].append(dict(fn=fn, deps=deps, group=None))
        self._record(('e', eng, idx), reads, writes)

    def dma(self, eng, fn, reads=(), writes=(), group=None, pool=None):
        if self.capture is not None:
            self.capture.append(('dma', (eng, fn, tuple(reads), tuple(writes), group, pool)))
            return None
        if group is None:
            group = self.dma_group(pool)
        deps = self._collect(reads, writes) | self._bar_deps(eng)
        deps.discard(('d', group))
        assert group not in self.closed, 'dma added to a group that already has waiters'
        for d in deps:
            if d[0] == 'd':
                self.closed.add(d[1])
        self.group_n[group] += 1
        self.ins[eng].append(dict(fn=fn, deps=deps, group=group))
        self._record(('d', group), reads, writes)
        return group

    def spacer(self, n):
        if self.capture is not None:
            self.capture.extend([('nop', None)] * n)

    def record(self, fn):
        assert self.capture is None
        self.capture = lst = []
        try:
            fn()
        finally:
            self.capture = None
        return lst

    def replay_merged(self, lists):
        lists = [l for l in lists if l]
        pos = [0] * len(lists)
        while True:
            best, bf = None, 2.0
            for i, l in enumerate(lists):
                if pos[i] < len(l):
                    f = pos[i] / len(l)
                    if f < bf:
                        best, bf = i, f
            if best is None:
                break
            kind, args = lists[best][pos[best]]
            pos[best] += 1
            if kind == 'op':
                self.op(*args)
            elif kind == 'dma':
                self.dma(*args)

    def barrier(self, exclude_groups=()):
        toks = set()
        for e in self.ENGS:
            for i in range(len(self.ins[e]) - 1, -1, -1):
                if self.ins[e][i]['group'] is None:
                    toks.add(('e', e, i))
                    break
        for g in range(len(self.group_n)):
            if self.group_n[g] > 0 and g not in exclude_groups:
                toks.add(('d', g))
        self.bar = toks
        self.bar_id += 1
        self.lastw = {}
        self.readers = {}
        self.ps_last = {}

    def _bar_deps(self, eng):
        if self.absorbed[eng] < self.bar_id:
            self.absorbed[eng] = self.bar_id
            return set(self.bar)
        return set()

    def emit(self, final_groups=()):
        nc = self.nc
        needed = {e: set() for e in self.ENGS}
        for e in self.ENGS:
            for ins in self.ins[e]:
                for d in ins['deps']:
                    if d is not None and d[0] == 'e':
                        if d[1] == e and (e == 'pe' or not self.self_sync):
                            continue
                        needed[d[1]].add(d[2])
        cnt = {}
        for e in self.ENGS:
            c = 0
            arr = []
            for i in range(len(self.ins[e])):
                if i in needed[e]:
                    c += 1
                arr.append(c)
            cnt[e] = arr
        with ExitStack() as st:
            esem = {e: st.enter_context(nc.semaphore('s_' + e)) for e in self.ENGS}
            POOLK = 4
            pools = {}
            gsem, gtarget = [], []
            for g in range(len(self.group_n)):
                pl = self.group_pool[g]
                if pl is None:
                    gsem.append(st.enter_context(nc.semaphore('g%d' % g)))
                    gtarget.append(16 * self.group_n[g])
                else:
                    if pl not in pools:
                        pools[pl] = dict(sems=[st.enter_context(nc.semaphore('p%s%d' % (pl, i))) for i in range(POOLK)],
                                         cnt=[0] * POOLK, nxt=0)
                    pd = pools[pl]
                    i = pd['nxt'] % POOLK
                    pd['nxt'] += 1
                    pd['cnt'][i] += 16 * self.group_n[g]
                    gsem.append(pd['sems'][i])
                    gtarget.append(pd['cnt'][i])
            block = st.enter_context(nc.Block())

            def run(e, engobj):
                seen = {x: -1 for x in self.ENGS}
                seeng = set()
                for i, ins in enumerate(self.ins[e]):
                    for d in sorted((x for x in ins['deps'] if x is not None), key=str):
                        if d[0] == 'e':
                            _, e2, i2 = d
                            if e2 == e and (e == 'pe' or not self.self_sync):
                                continue
                            if i2 > seen[e2]:
                                engobj.wait_ge(esem[e2], cnt[e2][i2])
                                seen[e2] = i2
                        else:
                            g = d[1]
                            if g not in seeng:
                                engobj.wait_ge(gsem[g], gtarget[g])
                                seeng.add(g)
                    r = ins['fn'](engobj)
                    if ins['group'] is not None:
                        r.then_inc(gsem[ins['group']], 16)
                    elif i in needed[e]:
                        r.then_inc(esem[e], 1)
                mine = [g for g in range(len(self.group_n)) if self.group_n[g] > 0]
                for g in mine:
                    engobj.wait_ge(gsem[g], gtarget[g])
                for e2 in self.ENGS:
                    if cnt[e2] and cnt[e2][-1] > 0:
                        engobj.wait_ge(esem[e2], cnt[e2][-1])

            @block.tensor
            def _(eng):
                run('pe', eng)

            @block.scalar
            def _(eng):
                run('act', eng)

            @block.vector
            def _(eng):
                run('dve', eng)

            @block.gpsimd
            def _(eng):
                run('pool', eng)

            @block.sync
            def _(eng):
                run('sp', eng)


class Arena:
    def __init__(self, base_ap, nbytes):
        self.base = base_ap
        self.nbytes = nbytes

    def view(self, off, shape, dt):
        esz = 2 if dt == BF16 else 4
        n = int(np.prod(shape[1:]))
        nb = n * esz
        assert off % 4 == 0 and off + nb <= self.nbytes, (off, nb, self.nbytes)
        nb4 = (nb + 3) // 4
        ap = self.base[:, off // 4: off // 4 + nb4]
        if dt == BF16:
            ap = ap.bitcast(BF16)[:, 0:n]
        if len(shape) > 2:
            names = ' '.join('d%d' % i for i in range(1, len(shape)))
            kw = {'d%d' % i: shape[i] for i in range(1, len(shape))}
            ap = ap.rearrange('p (%s) -> p %s' % (names, names), **kw)
        if shape[0] < 128:
            ap = ap[0:shape[0]]
        return ap


class Stack:
    def __init__(self, arena, lo, hi):
        self.a, self.lo, self.hi, self.cur = arena, lo, hi, lo

    def alloc(self, shape, dt):
        esz = 2 if dt == BF16 else 4
        nb = int(np.prod(shape[1:])) * esz
        nb = (nb + 31) // 32 * 32
        off = self.cur
        assert off + nb <= self.hi, ('arena overflow', off, nb, self.hi)
        self.cur += nb
        return self.a.view(off, shape, dt)


def host_consts():
    idx = np.arange(128)
    incl_f = (idx[:, None] <= idx[None, :]).astype(np.float32)
    strict_f = (idx[:, None] < idx[None, :]).astype(np.float32)
    incl_b = (idx[:, None] >= idx[None, :]).astype(np.float32)
    strict_b = (idx[:, None] > idx[None, :]).astype(np.float32)
    c = {}
    c['ident'] = np.eye(128, dtype=np.float32)
    c['tri'] = np.stack([incl_f, strict_f, strict_b, incl_b, strict_b, strict_f], 1)
    mm_f = np.concatenate([strict_f, incl_f, strict_f, incl_f], 1)
    mm_b = np.concatenate([strict_b, incl_b, strict_b, incl_b], 1)
    c['mmask'] = np.stack([mm_f, mm_b], 1)
    qt_f = np.concatenate([strict_f.T] * 4, 1)
    qt_b = np.concatenate([strict_b.T] * 4, 1)
    c['qtmask'] = np.stack([qt_f, qt_b], 1)
    oh = np.zeros((128, 2), np.float32)
    oh[127, 0] = 1.0
    oh[0, 1] = 1.0
    c['onehot'] = oh
    pidx = np.arange(128, dtype=np.float32)
    cnt = {0: (pidx + 1, pidx, 127 - pidx), 1: (128 - pidx, 127 - pidx, pidx)}
    cb = np.zeros((128, 2, 4), np.float32)
    for d_ in (0, 1):
        ci, cs_, cr = cnt[d_]
        cb[:, d_, 0] = 0.5 * CDEC * ci
        cb[:, d_, 1] = -0.5 * CDEC * ci
        cb[:, d_, 2] = 0.5 * CDEC * cs_
        cb[:, d_, 3] = 0.5 * CDEC * cr
    c['cbias'] = cb
    sel = np.zeros((128, 2, 128), np.float32)
    sel[0, 0, :] = 1.0
    sel[1, 1, :] = 1.0
    c['sel'] = sel
    t = np.arange(1024)
    quarter = 16
    inv = (10000.0 ** (-np.arange(quarter, dtype=np.float32) / quarter)).astype(np.float32)
    ang_r = (t // 64).astype(np.float32)[:, None] * inv[None, :]
    ang_c = (t % 64).astype(np.float32)[:, None] * inv[None, :]
    cos = np.stack([np.cos(ang_r), np.cos(ang_c)], 1).astype(np.float32)
    sin = np.stack([np.sin(ang_r), np.sin(ang_c)], 1).astype(np.float32)
    c['cos'] = cos.reshape(8, 128, 32).transpose(1, 0, 2).copy()
    c['sin'] = sin.reshape(8, 128, 32).transpose(1, 0, 2).copy()
    return c


CST_LAYOUT = [('ident', 128), ('tri', 768), ('onehot', 2), ('cbias', 8),
              ('sel', 256), ('cos', 256), ('sin', 256), ('mmask', 1024), ('qtmask', 1024)]
CST_COLS = sum(n for _, n in CST_LAYOUT)
CST_KEEP = CST_COLS - 2048


def pack_consts():
    c = host_consts()
    out = np.zeros((128, CST_COLS), np.float32)
    o = 0
    for name, n in CST_LAYOUT:
        out[:, o:o + n] = c[name].reshape(128, n)
        o += n
    return out


def build(stop_after=99, dbg=None):
    nc = bass.Bass("TRN2", target_bir_lowering=False)
    dt_in = lambda name, shape: nc.dram_tensor(name, shape, F32, kind="ExternalInput").ap()
    dt_out = lambda name, shape: nc.dram_tensor(name, shape, F32, kind="ExternalOutput").ap()
    xs_d = dt_in("xs", [NTOK, D])
    cvec_d = dt_in("cvec", [2, D])
    ck_d = dt_in("ck", [256, 128])
    cv_d = dt_in("cv", [256, 128])
    st_d = dt_in("st", [2, 8, 64, 64])
    wmod_d = dt_in("w_mod", [D, 6 * D])
    bmod_d = dt_in("b_mod", [1, 6 * D])
    nrm_d = dt_in("norms", [4, D])
    win_d = dt_in("w_in", [D, 2560])
    w0_d = dt_in("w0", [1, 1024])
    wup_d = dt_in("w_up", [2, 64, 512])
    a0_d = dt_in("a0", [1, 1024])
    aup_d = dt_in("a_up", [2, 64, 512])
    gup_d = dt_in("g_up", [128, 512])
    vecs_d = dt_in("vecs", [5, 512])
    qkn_d = dt_in("qkn", [2, 64])
    wout_d = dt_in("w_out", [D, D])
    fup_d = dt_in("ffn_up", [D, 2 * DFF])
    convc_d = dt_in("convc", [4, 2 * DFF])
    fdn_d = dt_in("ffn_down", [DFF, D])
    cst_d = dt_in("cst", [128, CST_COLS])
    y_d = dt_out("y", [NTOK, D])
    nk_d = dt_out("nk", [512, 128])
    nv_d = dt_out("nv", [512, 128])
    ns_d = dt_out("ns", [2, 2, 8, 64, 64])
    dbg_d = {}
    if dbg:
        for name, shape in dbg.items():
            dbg_d[name] = dt_out("dbg_" + name, list(shape))

    ARENA_BYTES = 207 * 1024
    with ExitStack() as st:
        arena_t = st.enter_context(nc.sbuf_tensor("arena", [128, ARENA_BYTES // 4], F32))
        ps = [st.enter_context(nc.psum_tensor("ps%d" % i, [128, 512], F32)) for i in range(8)]
        psb = [t[:].bitcast(BF16) for t in ps]
        A = Arena(arena_t[:], ARENA_BYTES)
        P = Prog(nc)
        outg = P.dma_group()

        def mm(out, lhsT, rhs, start, stop, r, w, skip=False):
            P.op('pe', lambda e: e.matmul(out, lhsT=lhsT, rhs=rhs, start=start, stop=stop, skip_group_check=skip), r, w)

        def tr(out, in_, idn, r, w):
            P.op('pe', lambda e: e.transpose(out=out, in_=in_, identity=idn), r, w)

        def act(out, in_, func, r, w, scale=None, bias=None, accum=None, eng='act'):
            kw = {}
            if scale is not None:
                kw['scale'] = scale
            if bias is not None:
                kw['bias'] = bias
            if accum is not None:
                kw['accum_out'] = accum
            P.op('act', lambda e: e.activation(out=out, in_=in_, func=func, **kw), r, w)

        def tt(eng, out, in0, in1, op, r, w):
            P.op(eng, lambda e: e.tensor_tensor(out=out, in0=in0, in1=in1, op=op), r, w)

        def ts(eng, out, in0, s1, s2, op0, op1, r, w):
            if op1 is None:
                P.op(eng, lambda e: e.tensor_scalar(out=out, in0=in0, scalar1=s1, scalar2=None, op0=op0), r, w)
            else:
                P.op(eng, lambda e: e.tensor_scalar(out=out, in0=in0, scalar1=s1, scalar2=s2, op0=op0, op1=op1), r, w)

        def stt(out, in0, scalar, in1, op0, op1, r, w):
            P.op('dve', lambda e: e.scalar_tensor_tensor(out=out, in0=in0, scalar=scalar, in1=in1, op0=op0, op1=op1), r, w)

        def cp(eng, out, in_, r, w):
            if eng == 'act':
                P.op('act', lambda e: e.activation(out=out, in_=in_, func=AF.Identity), r, w)
            else:
                P.op(eng, lambda e: e.tensor_copy(out=out, in_=in_), r, w)

        def red(out, in_, r, w, op=ALU.add):
            P.op('dve', lambda e: e.tensor_reduce(out=out, in_=in_, axis=AX.X, op=op), r, w)

        def ld(out, in_, w, r=(), group=None, eng='sp', pool=None, **kw):
            return P.dma(eng, lambda e: e.dma_start(out=out, in_=in_, **kw), r, w, group, pool)

        def rstd_of(ss, tmp, out, inv_n, eps, key, use_pow=True):
            ts('dve', tmp, ss, inv_n, eps, ALU.mult, ALU.add, [key + '.ss'], [key + '.tmp'])
            if not use_pow:
                act(tmp, tmp, AF.Sqrt, [key + '.tmp'], [key + '.tmp'])
                P.op('dve', lambda e: e.reciprocal(out=out, in_=tmp), [key + '.tmp'], [key + '.rstd'])
                return
            n = tmp.shape[-1]
            tt('pool', out, tmp, mhalf[:, 0:n], ALU.pow, [key + '.tmp', 'mhalf'], [key + '.rstd'])

        def dump(name, ap, keys):
            if name in dbg_d:
                ld(dbg_d[name], ap, [], keys)

        def finish():
            P.barrier()
            fin = A.view(PERS_END, [128, 64], F32)
            P.op('dve', lambda e: e.memset(fin, 0.0), [], ['fin'])
            ld(y_d[0:128, 0:64], fin, [], ['fin'])
            P.emit(final_groups=[outg])
            return nc

        CONST_END = 40 * 1024
        PERS_END = CONST_END + 76 * 1024
        YY_OFF = 183 * 1024
        sc = Stack(A, 0, CONST_END)
        cst = sc.alloc([128, CST_KEEP], F32)
        cstm = A.view(ARENA_BYTES - 8192, [128, 2048], F32)
        o = 0
        cv = {}
        for name, n in CST_LAYOUT:
            cv[name] = cst[:, o:o + n] if o < CST_KEEP else cstm[:, o - CST_KEEP:o - CST_KEEP + n]
            o += n
        identf = cv['ident']
        trif = cv['tri'].rearrange('p (a b) -> p a b', a=6)
        onehot = cv['onehot']
        cbias = cv['cbias'].rearrange('p (a b) -> p a b', a=2)
        sel = cv['sel'].rearrange('p (a b) -> p a b', a=2)
        cosf = cv['cos'].rearrange('p (t r f) -> p t r f', t=8, r=2)
        sinf = cv['sin'].rearrange('p (t r f) -> p t r f', t=8, r=2)
        ident = sc.alloc([128, 128], BF16)
        mmask = sc.alloc([128, 2, 512], BF16)
        qtmask = sc.alloc([128, 2, 512], BF16)
        ones_bf = sc.alloc([128, 128], BF16)
        tri = sc.alloc([128, 6, 128], BF16)
        modT = sc.alloc([128, 4, 8, 2], F32)
        gT = sc.alloc([128, 8, 2], F32)
        A1T = sc.alloc([128, 8, 2], F32)
        A2T = sc.alloc([128, 8, 2], F32)
        G1b = sc.alloc([128, 2, 1024], BF16)
        G2b = sc.alloc([128, 2, 1024], BF16)
        vecs_b = sc.alloc([128, 5, 512], F32)
        qkn_b = sc.alloc([128, 2, 64], F32)
        convT = sc.alloc([128, 44, 4], F32)
        lora_w = sc.alloc([128, 2, 512], BF16)
        w0a0 = sc.alloc([128, 1024], BF16)
        gup_bf = sc.alloc([128, 512], BF16)
        mhalf = sc.alloc([128, 8], F32)
        silucT = sc.alloc([128, 8, 2], BF16)

        ld(cst, cst_d[:, 0:CST_KEEP], ['cst'])
        ld(cstm, cst_d[:, CST_KEEP:CST_COLS], ['cstm'])
        cp('dve', ident, identf, ['cst'], ['ident'])
        cp('dve', mmask.rearrange('p a b -> p (a b)'), cv['mmask'], ['cstm'], ['mmask'])
        cp('dve', qtmask.rearrange('p a b -> p (a b)'), cv['qtmask'], ['cstm'], ['qtmask'])
        P.op('dve', lambda e: e.memset(ones_bf, 1.0), [], ['ones'])
        P.op('pool', lambda e: e.memset(mhalf, -0.5), [], ['mhalf'])
        cp('dve', tri, trif, ['cst'], ['tri'])
        g_c = P.dma_group()
        ld(vecs_b.rearrange('p a b -> p (a b)'), vecs_d.rearrange('a b -> (a b)').partition_broadcast(128), ['vecs'], group=g_c)
        ld(qkn_b.rearrange('p a b -> p (a b)'), qkn_d.rearrange('a b -> (a b)').partition_broadcast(128), ['qkn'], group=g_c)
        g_l = P.dma_group()
        ld(lora_w[0:64], wup_d.rearrange('d k n -> k d n'), ['lora'], group=g_l, eng='pool')
        ld(lora_w[64:128], aup_d.rearrange('d k n -> k d n'), ['lora'], group=g_l, eng='pool')
        ld(w0a0[0:1, :], w0_d, ['lora'], group=g_l, eng='pool')
        ld(w0a0[64:65, :], a0_d, ['lora'], group=g_l, eng='pool')
        ld(gup_bf, gup_d, ['lora'], group=g_l, eng='pool')

        if stop_after <= -3:
            P.emit(final_groups=[outg])
            return nc
        sp_ = Stack(A, CONST_END, PERS_END)
        r_p = sp_.alloc([128, NT, 512], BF16)
        v_p = sp_.alloc([128, NT, 512], BF16)
        k_p = sp_.alloc([128, NT, 512], BF16)
        kkn_p = sp_.alloc([128, NT, 512], BF16)
        txT = sp_.alloc([128, NTOK], BF16)
        sgT = sp_.alloc([128, NTOK], BF16)
        qT = sp_.alloc([128, 4, NTOK], BF16)
        kT_all = sp_.alloc([128, 1792], BF16)
        Vaug = sp_.alloc([128, 14, 2, 66], BF16)

        s0 = Stack(A, CONST_END, ARENA_BYTES - 8192)
        m_sb = s0.alloc([2, 6 * D], F32)
        c_sb = s0.alloc([2, D], F32)
        silu_c = s0.alloc([2, D], BF16)
        bmod_bf = s0.alloc([1, 6 * D], BF16)
        nrm_sb = s0.alloc([4, D], F32)
        gpost_b = s0.alloc([128, 2, D], F32)
        wm = [s0.alloc([128, 8, 512], BF16) for _ in range(2)]
        convrows = s0.alloc([4, 2 * DFF], F32)

        ld(c_sb, cvec_d, ['c_sb'])
        ld(bmod_bf, bmod_d, ['bmod'], eng='pool', max_dma_last_dim=2048)
        ld(nrm_sb, nrm_d, ['nrm'])
        ld(gpost_b[:, 0, :], nrm_d[2].partition_broadcast(128), ['gpost0'])
        ld(gpost_b[:, 1, :], nrm_d[3].partition_broadcast(128), ['gpost1'])
        act(silu_c, c_sb, AF.Silu, ['c_sb'], ['silu_c'])
        for c in range(8):
            tr(psb[0][:, c * 2:(c + 1) * 2], silu_c[0:2, c * 128:(c + 1) * 128], ident[0:2, 0:2], ['silu_c', 'ident'], ['ps0'])
        cp('dve', silucT.rearrange('p a b -> p (a b)'), psb[0][:, 0:16], ['ps0'], ['silucT'])
        if stop_after <= -2:
            P.emit(final_groups=[outg])
            return nc
        wmod_v = wmod_d.rearrange('(c p) n -> p c n', p=128)
        for j in range(4):
            b = wm[j % 2]
            kb = 'wm%d' % (j % 2)
            ld(b, wmod_v[:, :, j * 512:(j + 1) * 512], [kb], eng='pool', pool='w')
            pk = 'ps%d' % (1 + j % 2)
            pt = ps[1 + j % 2]
            for c in range(8):
                mm(pt[0:2, :], silucT[:, c, :], b[:, c, :], c == 0, False, ['silucT', kb], [pk])
            mm(pt[0:2, :], ones_bf[0:1, 0:2], bmod_bf[0:1, j * 512:(j + 1) * 512], False, True, ['ones', 'bmod'], [pk])
            cp('act' if j % 2 else 'dve', m_sb[:, j * 512:(j + 1) * 512], pt[0:2, :], [pk], ['m_sb'])
        if stop_after <= -1:
            dump('m', m_sb, ['m_sb'])
            P.emit(final_groups=[outg])
            return nc
        for ki, kind in enumerate((0, 1)):
            for c in range(8):
                i2 = (ki * 8 + c) * 2
                tr(ps[3][:, i2:i2 + 2], m_sb[0:2, kind * D + c * 128: kind * D + (c + 1) * 128], identf[0:2, 0:2], ['m_sb', 'cst'], ['ps3'])
        for c in range(8):
            i2 = 64 + c * 2
            tr(ps[3][:, i2:i2 + 2], nrm_sb[0:2, c * 128:(c + 1) * 128], identf[0:2, 0:2], ['nrm', 'cst'], ['ps3'])
        cp('dve', modT[:, 0:2].rearrange('p a b c -> p (a b c)'), ps[3][:, 0:32], ['ps3'], ['modT01'])
        cp('dve', gT.rearrange('p a b -> p (a b)'), ps[3][:, 64:80], ['ps3'], ['gT'])
        stt(A1T, modT[:, 1], 1.0, gT[:, :, 0:1].to_broadcast([128, 8, 2]), ALU.add, ALU.mult, ['modT01', 'gT'], ['A1T'])
        B1T = modT[:, 0]
        B2T = modT[:, 2]
        ld(convrows, convc_d, ['convrows'])
        for j in range(44):
            tr(ps[5][:, j * 4:(j + 1) * 4], convrows[0:4, j * 128:(j + 1) * 128], identf[0:4, 0:4], ['convrows', 'cst'], ['ps5'])
        cp('dve', convT.rearrange('p a b -> p (a b)'), ps[5][:, 0:176], ['ps5'], ['convT'])
        if dbg and 'm' in dbg:
            dump('m', m_sb, ['m_sb'])
        if dbg and 'A1T' in dbg:
            dump('A1T', A1T, ['A1T'])
        if stop_after <= 0:
            P.emit(final_groups=[outg])
            return nc

        P.barrier(exclude_groups=[outg])
        s1 = Stack(A, PERS_END, ARENA_BYTES)
        Win = s1.alloc([128, 8, 2560], BF16)
        xt = [s1.alloc([128, D], F32) for _ in range(2)]
        xsb = [s1.alloc([128, D], BF16) for _ in range(2)]
        hT = [s1.alloc([128, 8, 128], BF16) for _ in range(2)]
        junk = s1.alloc([128, D], F32)
        kkf = s1.alloc([128, 512], F32)
        sq2 = s1.alloc([128, 512], F32)
        sq3 = s1.alloc([128, 512], F32)
        qf = s1.alloc([128, 512], F32)
        qb = s1.alloc([128, 512], BF16)
        ta = s1.alloc([128, 256], F32)
        tb = s1.alloc([128, 256], F32)
        lo = s1.alloc([128, 256], BF16)
        kaf = s1.alloc([128, 128], F32)
        kab = s1.alloc([128, 128], BF16)
        vaf = s1.alloc([128, 128], F32)
        ckf = s1.alloc([128, 2, 128], F32)
        ckb = s1.alloc([128, 2, 128], BF16)
        stat = s1.alloc([128, 64], F32)
        qs = [s1.alloc([128, 512], F32) for _ in range(2)]
        ms = [s1.alloc([128, 512], F32) for _ in range(2)]

        win_v = win_d.rearrange('(c p) n -> p c n', p=128)
        dq = Win[:, :, 1536:2048].rearrange('p c (hh g d) -> p c hh g d', g=2, d=64)
        wl = [(Win[:, :, b3 * 512:(b3 + 1) * 512], win_v[:, :, b3 * 512:(b3 + 1) * 512]) for b3 in range(3)]
        for g in range(2):
            for hh in range(4):
                c0 = 1792 + (g * 4 + hh) * 64
                wl.append((dq[:, :, hh, g, :], win_v[:, :, c0:c0 + 64]))
        wl.append((Win[:, :, 2048:2304], win_v[:, :, 1536:1792]))
        wl.append((Win[:, :, 2304:2560], win_v[:, :, 2304:2560]))
        WINK = ['Win.%d' % i for i in range(len(wl))]
        for i, (dst, src) in enumerate(wl):
            ld(dst, src, [WINK[i]], [WINK[i - 4]] if i >= 4 else [], eng='pool', pool='w')
        P.op('pool', lambda e: e.memset(Vaug.rearrange('p a b c -> p (a b c)'), 1.0), [], ['Vaug'])
        ld(ckf, ck_d.rearrange('(a p) n -> p a n', p=128), ['ckf'])
        g_cv = P.dma_group()
        for a in range(2):
            ld(Vaug[:, 8 + a, :, 0:64], cv_d[a * 128:(a + 1) * 128, :].rearrange('p (g d) -> p g d', g=2), ['Vaug'], eng='pool', group=g_cv)
        cp('dve', ckb, ckf, ['ckf'], ['ckb'])
        for a in range(2):
            tr(psb[7][:, a * 128:(a + 1) * 128], ckb[:, a, :], ident, ['ckb', 'ident'], ['ps7'])
        cp('dve', kT_all[:, 1024:1280], psb[7][:, 0:256], ['ps7'], ['kT'])

        def rope(src, dst, nh, t, key):
            n = nh * 2 * 16
            sv = src.rearrange('p (h r x f) -> p h r x f', r=2, x=2, f=16)
            dv = dst.rearrange('p (h r x f) -> p h r x f', r=2, x=2, f=16)
            x1, x2 = sv[:, :, :, 0, :], sv[:, :, :, 1, :]
            cs = cosf[:, t].unsqueeze(1).to_broadcast([128, nh, 2, 16])
            sn = sinf[:, t].unsqueeze(1).to_broadcast([128, nh, 2, 16])
            tav = ta[:, 0:n].rearrange('p (h r f) -> p h r f', r=2, f=16)
            tbv = tb[:, 0:n].rearrange('p (h r f) -> p h r f', r=2, f=16)
            tt('pool', tav, x1, cs, ALU.mult, [key], ['ta'])
            tt('dve', tbv, x2, sn, ALU.mult, [key], ['tb'])
            tt('pool', dv[:, :, :, 0, :], tav, tbv, ALU.subtract, ['ta', 'tb'], [key + 'b'])
            tt('pool', tav, x1, sn, ALU.mult, [key], ['ta'])
            tt('dve', tbv, x2, cs, ALU.mult, [key], ['tb'])
            tt('pool', dv[:, :, :, 1, :], tav, tbv, ALU.add, ['ta', 'tb'], [key + 'b'])

        def p1_stageA(t):
            b = t % 2
            row = 0 if t < 8 else 1
            kx, ks, kh = 'xt%d' % b, 'xs%d' % b, 'hT%d' % b
            ld(xt[b], xs_d[t * 128:(t + 1) * 128, :], [kx], pool='x')
            act(junk, xt[b], AF.Square, [kx], ['junk', 'n1.ss'], accum=stat[:, 0:1])
            rstd_of(stat[:, 0:1], stat[:, 1:2], stat[:, 2:3], 1.0 / D, 1e-6, 'n1')
            ts('dve', xsb[b], xt[b], stat[:, 2:3], None, ALU.mult, None, [kx, 'n1.rstd'], [ks])
            for c in range(8):
                tr(psb[0][:, c * 128:(c + 1) * 128], xsb[b][:, c * 128:(c + 1) * 128], ident, [ks, 'ident'], ['ps0'])
            for c in range(8):
                act(hT[b][:, c, :], psb[0][:, c * 128:(c + 1) * 128], AF.Identity, ['ps0', 'A1T', 'modT01'], [kh],
                    scale=A1T[:, c, row:row + 1], bias=B1T[:, c, row:row + 1])
        def p1_stageA2(t):
            b = t % 2
            kh = 'hT%d' % b
            for blk in range(5):
                for c in range(8):
                    mm(ps[1 + blk][:, :], hT[b][:, c, :], Win[:, c, blk * 512:(blk + 1) * 512], c == 0, c == 7, [kh] + WINK, ['ps%d' % (1 + blk)])
            kq, km = 'qs%d' % b, 'ms%d' % b
            cp('act', r_p[:, t, :], ps[1][:, :], ['ps1'], ['r_p'])
            cp('act', k_p[:, t, :], ps[2][:, :], ['ps2'], ['k_p'])
            cp('act', v_p[:, t, :], ps[3][:, :], ['ps3'], ['v_p'])
            cp('act', qs[b], ps[4][:, :], ['ps4'], [kq])
            cp('act', ms[b], ps[5][:, :], ['ps5'], [km])
        def p1_stageB(t):
            b = t % 2
            row = 0 if t < 8 else 1
            kq, km = 'qs%d' % b, 'ms%d' % b
            tt('dve', kkf, k_p[:, t, :], vecs_b[:, 0, :], ALU.mult, ['k_p', 'vecs'], ['kkf'])
            tt('dve', sq2, kkf, kkf, ALU.mult, ['kkf'], ['sq2'])
            red(stat[:, 8:16], sq2.rearrange('p (h d) -> p h d', d=64), ['sq2'], ['nk.ss'])
            rstd_of(stat[:, 8:16], stat[:, 16:24], stat[:, 24:32], 1.0, 1e-12, 'nk')
            tt('pool', kkn_p[:, t, :].rearrange('p (h d) -> p h d', d=64), kkf.rearrange('p (h d) -> p h d', d=64),
               stat[:, 24:32].unsqueeze(2).to_broadcast([128, 8, 64]), ALU.mult, ['kkf', 'nk.rstd'], ['kkn_p'])
            tt('dve', sq3, qs[b], qs[b], ALU.mult, [kq], ['sq3'])
            red(stat[:, 32:40], sq3.rearrange('p (h d) -> p h d', d=64), ['sq3'], ['nq.ss'])
            rstd_of(stat[:, 32:40], stat[:, 40:48], stat[:, 48:56], 1.0 / 64, 1e-6, 'nq')
            tt('dve', qf.rearrange('p (h d) -> p h d', d=64), qs[b].rearrange('p (h d) -> p h d', d=64),
               stat[:, 48:56].unsqueeze(2).to_broadcast([128, 8, 64]), ALU.mult, [kq, 'nq.rstd'], ['qf'])
            tt('pool', qf.rearrange('p (h d) -> p h d', d=64), qf.rearrange('p (h d) -> p h d', d=64),
               qkn_b[:, 0:1, :].to_broadcast([128, 8, 64]), ALU.mult, ['qf', 'qkn'], ['qf'])
            if t < 8:
                rope(qf, qb, 8, t, 'qf')
            else:
                cp('pool', qb, qf, ['qf'], ['qfb'])
            for hp in range(4):
                tr(psb[6][:, hp * 128:(hp + 1) * 128], qb[:, hp * 128:(hp + 1) * 128], ident, ['qfb', 'ident'], ['ps6'])
            cp('dve', qT[:, :, t * 128:(t + 1) * 128], psb[6][:, 0:512].rearrange('p (a b) -> p a b', b=128), ['ps6'], ['qT'])
            act(lo[:, 0:64], ms[b][:, 0:64], AF.Tanh, [km], ['lo'])
            cp('dve', lo[:, 64:128], ms[b][:, 64:128], [km], ['lo'])
            act(lo[:, 128:256], ms[b][:, 128:256], AF.Sigmoid, [km], ['lo'])
            tr(psb[7][:, 0:128], lo[:, 0:128], ident, ['lo', 'ident'], ['ps7'])
            tr(psb[7][:, 128:256], lo[:, 128:256], ident, ['lo', 'ident'], ['ps7'])
            cp('dve', txT[:, t * 128:(t + 1) * 128], psb[7][:, 0:128], ['ps7'], ['txT'])
            cp('dve', sgT[:, t * 128:(t + 1) * 128], psb[7][:, 128:256], ['ps7'], ['sgT'])
            tt('pool', sq3[:, 0:128], ms[b][:, 256:384], ms[b][:, 256:384], ALU.mult, [km, 'sq3'], ['sq3'])
            red(stat[:, 56:58], sq3[:, 0:128].rearrange('p (h d) -> p h d', d=64), ['sq3'], ['na.ss'])
            rstd_of(stat[:, 56:58], stat[:, 58:60], stat[:, 60:62], 1.0 / 64, 1e-6, 'na')
            tt('dve', kaf.rearrange('p (h d) -> p h d', d=64), ms[b][:, 256:384].rearrange('p (h d) -> p h d', d=64),
               stat[:, 60:62].unsqueeze(2).to_broadcast([128, 2, 64]), ALU.mult, [km, 'na.rstd'], ['kaf'])
            tt('pool', kaf.rearrange('p (h d) -> p h d', d=64), kaf.rearrange('p (h d) -> p h d', d=64),
               qkn_b[:, 1:2, :].to_broadcast([128, 2, 64]), ALU.mult, ['kaf', 'qkn'], ['kaf'])
            if t < 8:
                rope(kaf, kab, 2, t, 'kaf')
                kcol = t * 128
                vt = t
            else:
                cp('pool', kab, kaf, ['kaf'], ['kafb'])
                sq_i = 1 if t < 10 else 2
                lt = (t - 8) % 2
                kcol = KOFF[sq_i] + lt * 128
                vt = VT0[sq_i] + lt
                ld(nk_d[(t - 8) * 128:(t - 7) * 128, :], kaf, [], ['kaf'], pool='o')
                ld(nv_d[(t - 8) * 128:(t - 7) * 128, :], ms[b][:, 384:512], [], [km], pool='o')
            tr(psb[7][:, 256:384], kab, ident, ['kafb', 'ident'], ['ps7'])
            cp('dve', kT_all[:, kcol:kcol + 128], psb[7][:, 256:384], ['ps7'], ['kT'])
            cp('pool', Vaug[:, vt, :, 0:64], ms[b][:, 384:512].rearrange('p (g d) -> p g d', d=64), [km], ['Vaug'])
        def run_pipeline(stages, n):
            for k in range(n + len(stages) - 1):
                lists = []
                for si_, f in enumerate(stages):
                    it = k - si_
                    if 0 <= it < n:
                        lists.append(P.record(lambda f=f, it=it: f(it)))
                P.replay_merged(lists)

        run_pipeline([p1_stageA, p1_stageA2, p1_stageB], NT)
        if stop_after <= 1:
            for name, (ap_, key) in dict(r_p=(r_p, 'r_p'), kkn_p=(kkn_p, 'kkn_p'), qT=(qT, 'qT'), kT=(kT_all, 'kT'),
                                         Vaug=(Vaug, 'Vaug'), txT=(txT, 'txT'), sgT=(sgT, 'sgT'), v_p=(v_p, 'v_p')).items():
                if name in dbg_d:
                    shp = list(ap_.shape)
                    n = int(np.prod(shp[1:]))
                    flat = ap_
                    if len(shp) == 3:
                        flat = ap_.rearrange('p a b -> p (a b)')
                    elif len(shp) == 4:
                        flat = ap_.rearrange('p a b c -> p (a b c)')
                    P.barrier()
                    dbgf = A.view(PERS_END, [128, n], F32)
                    cp('dve', dbgf, flat, [key], ['dbgf'])
                    dump(name, dbgf, ['dbgf'])
            return finish()
        P.barrier(exclude_groups=[outg])
        yat = A.view(YY_OFF, [128, NT, 512], BF16)
        yrw = A.view(YY_OFF + 12288, [128, NT, 512], BF16)
        s2 = Stack(A, PERS_END, YY_OFF)
        PT = [s2.alloc([128, 512], BF16) for _ in range(3)]
        rec = [s2.alloc([128, 4], F32) for _ in range(2)]
        items = []
        for si, (t0, ntile) in enumerate(SEQS):
            nchunk = 2 if ntile == 8 else 1
            ntc = ntile // nchunk
            for pos in range(8):
                for ch in range(nchunk):
                    for kt in range(NKT[si]):
                        items.append((si, pos, ch, kt, ntc, t0 + ch * ntc))

        def s_mm(i):
            si, pos, ch, kt, ntc, tile0 = items[i]
            hp, g = pos // 2, pos % 2
            b = i % 3
            gs = slice(g * 64, (g + 1) * 64)
            mm(ps[b][:, 0:ntc * 128], kT_all[gs, KOFF[si] + kt * 128:KOFF[si] + (kt + 1) * 128],
               qT[gs, hp, tile0 * 128:(tile0 + ntc) * 128], True, True, ['kT', 'qT'], ['ps%d' % b])

        m2 = s2.alloc([2, 4 * D], F32)
        wm2 = [s2.alloc([128, 8, 512], BF16) for _ in range(2)]
        bmod2 = s2.alloc([1, 4 * D], BF16)
        gpost_b2 = s2.alloc([128, 2, D], F32)

        def mod_rest():
            ld(bmod2, bmod_d[:, 2 * D:6 * D], ['bmod2'], eng='pool', max_dma_last_dim=2048)
            ld(gpost_b2[:, 0, :], nrm_d[2].partition_broadcast(128), ['gpostb0'])
            ld(gpost_b2[:, 1, :], nrm_d[3].partition_broadcast(128), ['gpostb1'])
            wmod_v2 = wmod_d.rearrange('(c p) n -> p c n', p=128)
            ld(wm2[0], wmod_v2[:, :, 4 * 512:5 * 512], ['wm20'], eng='pool', pool='w')
            for j in range(4, 12):
                bw = wm2[j % 2]
                kb = 'wm2%d' % (j % 2)
                if j + 1 < 12:
                    ld(wm2[(j + 1) % 2], wmod_v2[:, :, (j + 1) * 512:(j + 2) * 512], ['wm2%d' % ((j + 1) % 2)], eng='pool', pool='w')
                pk = 'ps%d' % (5 + j % 2)
                pt = ps[5 + j % 2]
                for c in range(8):
                    mm(pt[0:2, :], silucT[:, c, :], bw[:, c, :], c == 0, False, ['silucT', kb], [pk])
                mm(pt[0:2, :], ones_bf[0:1, 0:2], bmod2[0:1, (j - 4) * 512:(j - 3) * 512], False, True, ['ones', 'bmod2'], [pk])
                cp('dve', m2[:, (j - 4) * 512:(j - 3) * 512], pt[0:2, :], [pk], ['m2'])
            for ki, kofs in enumerate((1, 2)):
                for c in range(8):
                    i2 = (ki * 8 + c) * 2
                    tr(ps[7][:, i2:i2 + 2], m2[0:2, kofs * D + c * 128: kofs * D + (c + 1) * 128], identf[0:2, 0:2], ['m2', 'cst'], ['ps7'])
            cp('dve', modT[:, 2:4].rearrange('p a b c -> p (a b c)'), ps[7][:, 0:32], ['ps7'], ['modT'])
            stt(A2T, modT[:, 3], 1.0, gT[:, :, 1:2].to_broadcast([128, 8, 2]), ALU.add, ALU.mult, ['modT', 'gT'], ['A2T'])
            for (Gb, kofs, gi, key) in ((G1b, 0, 0, 'G1b'), (G2b, 3, 1, 'G2b')):
                for row in range(2):
                    for blk in range(2):
                        pk = 'ps%d' % (5 + blk)
                        mm(ps[5 + blk][:, :], sel[0:2, row, :], m2[0:2, kofs * D + blk * 512: kofs * D + (blk + 1) * 512], True, True, ['cst', 'm2'], [pk])
                        tt('dve', Gb[:, row, blk * 512:(blk + 1) * 512], ps[5 + blk][:, :], gpost_b2[:, gi, blk * 512:(blk + 1) * 512], ALU.mult, [pk, 'gpostb%d' % gi], [key])

        l_mod = P.record(mod_rest)
        P.capture = l_att = []
        grp = 0
        s_mm(0)
        for i, (si, pos, ch, kt, ntc, tile0) in enumerate(items):
            hp, g = pos // 2, pos % 2
            b = i % 3
            if i + 1 < len(items):
                s_mm(i + 1)
            act(PT[b][:, 0:ntc * 128], ps[b][:, 0:ntc * 128], AF.Exp, ['ps%d' % b], ['PT%d' % b], scale=0.125)
            ob = 3 + grp % 2
            for j in range(ntc):
                mm(ps[ob][:, j * 65:(j + 1) * 65], PT[b][:, j * 128:(j + 1) * 128], Vaug[:, VT0[si] + kt, g, 0:65],
                   kt == 0 and j == 0, kt == NKT[si] - 1 and j == ntc - 1, ['PT%d' % b, 'Vaug'], ['ps%d' % ob], skip=True)
            if kt == NKT[si] - 1:
                ov = ps[ob][:, 0:ntc * 65].rearrange('p (j c) -> p j c', c=65)
                rc = rec[grp % 2]
                P.op('dve', lambda e, o_=rc[:, 0:ntc].unsqueeze(2), i_=ov[:, :, 64:65]: e.reciprocal(out=o_, in_=i_), ['ps%d' % ob], ['rec%d' % (grp % 2)])
                tt('dve', yat[:, tile0:tile0 + ntc, pos * 64:(pos + 1) * 64], ov[:, :, 0:64],
                   rc[:, 0:ntc].unsqueeze(2).to_broadcast([128, ntc, 64]), ALU.mult, ['ps%d' % ob, 'rec%d' % (grp % 2)], ['yat'])
                grp += 1
        P.capture = None
        P.replay_merged([l_att, l_mod])
        if stop_after <= 2:
            P.barrier()
            dbgf = A.view(PERS_END, [128, NT * 512], F32)
            cp('dve', dbgf, yat.rearrange('p a b -> p (a b)'), ['yat'], ['dbgf'])
            dump('yat', dbgf, ['dbgf'])
            return finish()
        P.barrier(exclude_groups=[outg])
        s3 = Stack(A, PERS_END, YY_OFF)
        s3b = Stack(A, CONST_END + 4 * 12288 + 2 * 3072, PERS_END)
        sg = s3.alloc([128, 512], F32)
        al = s3.alloc([128, 512], BF16)
        e_ex = s3.alloc([128, 512], F32)
        e_ng = s3.alloc([128, 512], F32)
        e_rm = s3.alloc([128, 512], F32)
        at_ = s3.alloc([128, 512], BF16)
        rt_ = s3.alloc([128, 512], BF16)
        bt_ = s3.alloc([128, 512], BF16)
        kt_ = s3.alloc([128, 512], BF16)
        bb = s3.alloc([128, 512], BF16)
        kka = s3.alloc([128, 512], BF16)
        kd = s3.alloc([128, 512], F32)
        rrk = s3.alloc([128, 512], F32)
        bkT = s3.alloc([128, 2, 4, 128], BF16)
        Q0T = s3.alloc([128, 8, 128], BF16)
        Qb = [s3.alloc([128, 8, 128], BF16) for _ in range(2)]
        QTb = [s3.alloc([128, 8, 128], BF16) for _ in range(2)]
        Rb = [s3.alloc([128, 8, 128], BF16) for _ in range(2)]
        Xb = s3.alloc([128, 512], BF16)
        Ub = s3.alloc([128, 512], BF16)
        ysum = s3.alloc([128, 512], F32)
        sqy = s3.alloc([128, 512], F32)
        ST = s3.alloc([128, 4, 64], F32)
        STb = s3.alloc([128, 4, 64], BF16)
        Pc = s3.alloc([128, 4], F32)
        stl = s3.alloc([64, 8, 64], F32)
        bsb = s3.alloc([128, NT, 8], F32)
        st3 = s3.alloc([128, 64], F32)
        bsf3 = s3.alloc([128, 3, 8], F32)
        arT2 = [s3.alloc([128, 4, 2, 128], BF16), s3b.alloc([128, 4, 2, 128], BF16)]
        AB2 = [s3.alloc([128, 8, 2, 128], BF16), s3b.alloc([128, 8, 2, 128], BF16)]
        AK2 = [s3.alloc([128, 8, 2, 128], BF16), s3b.alloc([128, 8, 2, 128], BF16)]
        TT2 = [s3b.alloc([128, 8, 128], BF16) for _ in range(2)]
        bh3 = [s3.alloc([128, 512], BF16), s3.alloc([128, 512], BF16), s3b.alloc([128, 512], BF16)]
        kh3 = [s3.alloc([128, 512], BF16), s3.alloc([128, 512], BF16), s3b.alloc([128, 512], BF16)]
        ein3 = [s3.alloc([128, 512], F32), s3.alloc([128, 512], F32), s3b.alloc([128, 512], F32)]
        gsb = s3b.alloc([128, 512], BF16)
        sgh = s3.alloc([128, 512], BF16)
        sgl = s3.alloc([128, 512], BF16)
        flat3 = lambda ap: ap.rearrange('p a b -> p (a b)')
        flat4 = lambda ap: ap.rearrange('p a b c -> p (a b c)')
        hv = lambda ap: ap.rearrange('p (h d) -> p h d', d=64)

        def prepA(t, d, p3):
            tok = slice(t * 128, (t + 1) * 128)
            e_in_, bh_, kh_ = ein3[p3], bh3[p3], kh3[p3]
            ke, kb, kk_ = 'e_in%d' % p3, 'bh%d' % p3, 'kh%d' % p3
            mm(ps[6][:, :], txT[0:64, tok], lora_w[0:64, d, :], True, False, ['txT', 'lora'], ['ps6'])
            mm(ps[6][:, :], ones_bf[0:1, 0:128], w0a0[0:1, d * 512:(d + 1) * 512], False, True, ['ones', 'lora'], ['ps6'])
            yield
            act(sg, ps[6][:, :], AF.Tanh, ['ps6'], ['sg'], scale=0.5)
            lt_eng = 'pool' if d == 1 else 'dve'
            cp(lt_eng, sgh, sg, ['sg'], ['sgh'])
            yield
            mm(ps[6][:, :], txT[64:128, tok], lora_w[64:128, d, :], True, False, ['txT', 'lora'], ['ps6'])
            mm(ps[6][:, :], ones_bf[64:65, 0:128], w0a0[64:65, d * 512:(d + 1) * 512], False, True, ['ones', 'lora'], ['ps6'])
            tt('pool', kka, k_p[:, t, :], vecs_b[:, 1, :], ALU.mult, ['k_p', 'vecs'], ['kka'])
            tt(lt_eng, sgl, sg, sgh, ALU.subtract, ['sg', 'sgh'], ['sgl'])
            yield
            act(kd, ps[6][:, :], AF.Tanh, ['ps6'], ['kd'], scale=0.5)
            ts('dve', al, kd, 0.5, 0.5, ALU.mult, ALU.add, ['kd'], ['al'])
            tt('pool', rrk, r_p[:, t, :], vecs_b[:, 2, :], ALU.mult, ['r_p', 'vecs'], ['rrk'])
            yield
            mm(ps[6][:, :], tri[:, 3 * d + 0, :], sgh, True, False, ['tri', 'sgh'], ['ps6'])
            mm(ps[6][:, :], tri[:, 3 * d + 0, :], sgl, False, True, ['tri', 'sgl'], ['ps6'])
            tt('pool', bb, kkn_p[:, t, :], al, ALU.mult, ['kkn_p', 'al'], ['bb'])
            stt(kd, al, -1.0, kka, ALU.add, ALU.mult, ['al', 'kka'], ['kd'])
            yield
            act(e_in_, ps[6][:, :], AF.Exp, ['ps6', 'cst'], [ke], scale=0.5 * CDEC, bias=cbias[:, d, 0:1])
            act(e_ng, ps[6][:, :], AF.Exp, ['ps6', 'cst'], ['e_ng'], scale=-0.5 * CDEC, bias=cbias[:, d, 1:2])
            tt('pool', kd, kd, k_p[:, t, :], ALU.add, ['kd', 'k_p'], ['kd'])
            yield
            mm(ps[6][:, :], tri[:, 3 * d + 1, :], sgh, True, False, ['tri', 'sgh'], ['ps6'])
            mm(ps[6][:, :], tri[:, 3 * d + 1, :], sgl, False, True, ['tri', 'sgl'], ['ps6'])
            tt('pool', rt_, r_p[:, t, :], e_in_, ALU.mult, ['r_p', ke], ['rt_'])
            tt('pool', bt_, bb, e_ng, ALU.mult, ['bb', 'e_ng'], ['bt_'])
            tt(lt_eng, kt_, kd, e_ng, ALU.mult, ['kd', 'e_ng'], ['kt_'])
            yield
            act(e_ex, ps[6][:, :], AF.Exp, ['ps6', 'cst'], ['e_ex'], scale=0.5 * CDEC, bias=cbias[:, d, 2:3])
            tt('dve', rrk, rrk, kd, ALU.mult, ['rrk', 'kd'], ['rrk'])
            yield
            mm(ps[6][:, :], tri[:, 3 * d + 2, :], sgh, True, False, ['tri', 'sgh'], ['ps6'])
            mm(ps[6][:, :], tri[:, 3 * d + 2, :], sgl, False, True, ['tri', 'sgl'], ['ps6'])
            stt(at_, e_ex, -1.0, kkn_p[:, t, :], ALU.mult, ALU.mult, ['e_ex', 'kkn_p'], ['at_'])
            if d == 1:
                red(bsb[:, t, :], hv(rrk), ['rrk'], ['bsb'])
            else:
                red(bsf3[:, p3, :], hv(rrk), ['rrk'], ['bsf%d' % p3])
            yield
            act(e_rm, ps[6][:, :], AF.Exp, ['ps6', 'cst'], ['e_rm'], scale=0.5 * CDEC, bias=cbias[:, d, 3:4])
            tt('pool', bh_, bb, e_rm, ALU.mult, ['bb', 'e_rm'], [kb])
            tt('pool', kh_, kd, e_rm, ALU.mult, ['kd', 'e_rm'], [kk_])
            yield

        def prepB(t, d, p2, nxt):
            arT, AB, AK, TT = arT2[p2], AB2[p2], AK2[p2], TT2[p2]
            kar, kab_, kak = 'arT%d' % p2, 'AB%d' % p2, 'AK%d' % p2
            for hp in range(4):
                cs = slice(hp * 128, (hp + 1) * 128)
                b_ar = ps[hp // 2]
                o = (hp % 2) * 256
                mm(b_ar[:, o:o + 128], at_[:, cs], ident, True, True, ['at_', 'ident'], ['ps%d' % (hp // 2)])
                mm(b_ar[:, o + 128:o + 256], rt_[:, cs], ident, True, True, ['rt_', 'ident'], ['ps%d' % (hp // 2)])
                mm(ps[2][:, cs], bt_[:, cs], ident, True, True, ['bt_', 'ident'], ['ps2'])
                mm(ps[3][:, cs], kt_[:, cs], ident, True, True, ['kt_', 'ident'], ['ps3'])
            fa = flat4(arT)
            cp('act', fa[:, 0:512], ps[0][:, :], ['ps0'], [kar])
            cp('act', fa[:, 512:1024], ps[1][:, :], ['ps1'], [kar])
            cp('act', flat3(bkT[:, 0]), ps[2][:, :], ['ps2'], ['bkT'])
            cp('act', flat3(bkT[:, 1]), ps[3][:, :], ['ps3'], ['bkT'])
            if nxt is not None:
                next(nxt, None)
            for half in range(2):
                for i in range(4):
                    h = half * 4 + i
                    hp, e = h // 2, h % 2
                    es = slice(e * 64, (e + 1) * 64)
                    c2 = slice((i // 2) * 256, (i // 2) * 256 + 256)
                    mm(ps[e][:, c2], bkT[es, 0, hp, :], arT[es, hp], True, True, ['bkT', kar], ['ps%d' % e])
                    mm(ps[2 + e][:, c2], bkT[es, 1, hp, :], arT[es, hp], True, True, ['bkT', kar], ['ps%d' % (2 + e)])
                    mm(ps[4 + e][:, (i // 2) * 128:(i // 2 + 1) * 128], arT[es, hp, 0, :], bkT[es, 0, hp, :], True, True, ['bkT', kar], ['ps%d' % (4 + e)])
                m2 = mmask[:, d, :].rearrange('p (a x) -> p a x', x=256)
                q2 = qtmask[:, d, 0:256].rearrange('p (a x) -> p a x', x=128)
                for e in range(2):
                    h0 = half * 4 + e
                    tt('dve', AB[:, h0:h0 + 3:2].rearrange('p a b c -> p a (b c)'), ps[e][:, :].rearrange('p (a x) -> p a x', x=256), m2,
                       ALU.mult, ['ps%d' % e, 'mmask'], [kab_ + '.%d' % half])
                    tt('dve', Q0T[:, h0:h0 + 3:2], ps[4 + e][:, 0:256].rearrange('p (a x) -> p a x', x=128), q2,
                       ALU.mult, ['ps%d' % (4 + e), 'qtmask'], ['Q0T.%d' % half])
                for e in range(2):
                    h0 = half * 4 + e
                    tt('dve', AK[:, h0:h0 + 3:2].rearrange('p a b c -> p a (b c)'), ps[2 + e][:, :].rearrange('p (a x) -> p a x', x=256), m2,
                       ALU.mult, ['ps%d' % (2 + e), 'mmask'], [kak])
                if nxt is not None:
                    next(nxt, None)
            for j in range(2):
                tt('pool', Rb[0][:, j * 4:j * 4 + 4], AB[:, j * 4:j * 4 + 4, 0, :], ident.unsqueeze(1).to_broadcast([128, 4, 128]), ALU.add,
                   [kab_ + '.%d' % j, 'ident'], ['R0.%d' % j])
            Qp, QTp = AB[:, :, 0, :], Q0T
            kq = [kab_ + '.0', kab_ + '.1']
            kqt = ['Q0T.0', 'Q0T.1']

            def r_level(lev, QTl, kqtl):
                Rp = Rb[(lev - 1) % 2]
                Rn = TT if lev == 6 else Rb[lev % 2]
                kn = (lambda j: 'TT%d.%d' % (p2, j)) if lev == 6 else (lambda j: 'R%d.%d' % (lev % 2, j))
                for h in range(8):
                    j = h // 4
                    mm(ps[4 + j][:, (h % 4) * 128:(h % 4 + 1) * 128], QTl[:, h, :], Rp[:, h, :], True, True,
                       [kqtl[j], 'R%d.%d' % ((lev - 1) % 2, j)], ['ps%d' % (4 + j)])
                for j in range(2):
                    tt('dve', flat3(Rn[:, j * 4:j * 4 + 4]), ps[4 + j][:, :], flat3(Rp[:, j * 4:j * 4 + 4]), ALU.add,
                       ['ps%d' % (4 + j), 'R%d.%d' % ((lev - 1) % 2, j)], [kn(j)])

            pend = None
            for lev in range(1, 7):
                pi = lev % 2
                nkq = ['Q%d.%d' % (pi, j) for j in range(2)]
                nkqt = ['QT%d.%d' % (pi, j) for j in range(2)]
                for j in range(2):
                    if lev <= 5:
                        for h in range(j * 4, j * 4 + 4):
                            mm(ps[j][:, (h % 4) * 128:(h % 4 + 1) * 128], QTp[:, h, :], Qp[:, h, :], True, True, [kq[j], kqt[j]], ['ps%d' % j])
                    for h in range(j * 4, j * 4 + 4):
                        mm(ps[2 + j][:, (h % 4) * 128:(h % 4 + 1) * 128], Qp[:, h, :], QTp[:, h, :], True, True, [kq[j], kqt[j]], ['ps%d' % (2 + j)])
                    if lev <= 5:
                        cp('act', flat3(Qb[pi][:, j * 4:j * 4 + 4]), ps[j][:, :], ['ps%d' % j], [nkq[j]])
                    cp('act', flat3(QTb[pi][:, j * 4:j * 4 + 4]), ps[2 + j][:, :], ['ps%d' % (2 + j)], [nkqt[j]])
                if pend is not None:
                    r_level(*pend)
                Qp, QTp, kq, kqt = Qb[pi], QTb[pi], nkq, nkqt
                pend = (lev, QTp, kqt)
                if nxt is not None:
                    next(nxt, None)
            r_level(*pend)
            if nxt is not None:
                for _ in nxt:
                    pass

        def chain(stp, p2, p3):
            t, d, si = stp['t'], stp['d'], stp['si']
            arT, AB, AK, TT = arT2[p2], AB2[p2], AK2[p2], TT2[p2]
            kar, kab_, kak = 'arT%d' % p2, 'AB%d' % p2, 'AK%d' % p2
            ktt = ['TT%d.0' % p2, 'TT%d.1' % p2]
            e_in_, bh_, kh_ = ein3[p3], bh3[p3], kh3[p3]
            ke, kb, kk_ = 'e_in%d' % p3, 'bh%d' % p3, 'kh%d' % p3
            if stp['first']:
                if si == 0:
                    ld(stl, st_d[d].rearrange('h v k -> v h k'), ['stl'])
                    for hp in range(4):
                        tr(ps[7][:, hp * 64:(hp + 1) * 64], flat3(stl[:, 2 * hp:2 * hp + 2, :]), identf[0:64, 0:64], ['stl', 'cst'], ['ps7'])
                    cp('dve', flat3(ST), ps[7][:, 0:256], ['ps7'], ['ST'])
                    cp('act', flat3(STb), ps[7][:, 0:256], ['ps7'], ['STb'])
                else:
                    P.op('dve', lambda e: e.memset(flat3(ST), 0.0), [], ['ST'])
                    P.op('pool', lambda e: e.memset(flat3(STb), 0.0), [], ['STb'])
            for hp in range(4):
                mm(ps[7][:, hp:hp + 1], e_in_[:, hp * 128:(hp + 1) * 128], onehot[:, d:d + 1], True, True, [ke, 'cst'], ['ps7'])
            cp('act', Pc, ps[7][:, 0:4], ['ps7'], ['Pc'])
            if d == 0:
                tok = slice(t * 128, (t + 1) * 128)
                mm(ps[7][:, :], sgT[:, tok], gup_bf, True, True, ['sgT', 'lora'], ['ps7'])
                cp('act', gsb, ps[7][:, :], ['ps7'], ['gsb'])
            for h in range(8):
                hp, e = h // 2, h % 2
                es = slice(e * 64, (e + 1) * 64)
                hc = slice(h * 64, (h + 1) * 64)
                mm(ps[7][:, hc], arT[es, hp, 0, :], STb[es, hp, :], True, False, [kar, 'STb'], ['ps7'])
                mm(ps[7][:, hc], AK[:, h, 0, :], v_p[:, t, hc], False, True, [kak, 'v_p'], ['ps7'])
            cp('act', Xb, ps[7][:, :], ['ps7'], ['Xb'])
            P.spacer(10)
            for h in range(8):
                hc = slice(h * 64, (h + 1) * 64)
                mm(ps[7][:, hc], TT[:, h, :], Xb[:, hc], True, True, ktt + ['Xb'], ['ps7'])
            cp('act', Ub, ps[7][:, :], ['ps7'], ['Ub'])
            P.spacer(10)
            for h in range(8):
                hp, e = h // 2, h % 2
                es = slice(e * 64, (e + 1) * 64)
                hc = slice(h * 64, (h + 1) * 64)
                mm(ps[7][:, hc], arT[es, hp, 1, :], STb[es, hp, :], True, False, [kar, 'STb'], ['ps7'])
                mm(ps[7][:, hc], AB[:, h, 1, :], Ub[:, hc], False, False, [kab_ + '.0', kab_ + '.1', 'Ub'], ['ps7'])
                mm(ps[7][:, hc], AK[:, h, 1, :], v_p[:, t, hc], False, True, [kak, 'v_p'], ['ps7'])
            if d == 1:
                cp('act', yrw[:, t, :], ps[7][:, :], ['ps7'], ['yrw'])
            else:
                tt('dve', ysum, ps[7][:, :], yrw[:, t, :], ALU.add, ['ps7', 'yrw'], ['ysum'])
            P.spacer(8)
            for hp in range(4):
                pc = slice(hp * 128, (hp + 1) * 128)
                mm(ps[7][:, pc], bh_[:, pc], Ub[:, pc], True, False, [kb, 'Ub'], ['ps7'])
                mm(ps[7][:, pc], kh_[:, pc], v_p[:, t, pc], False, True, [kk_, 'v_p'], ['ps7'])
            tt('dve', ST, ST, Pc.unsqueeze(2).to_broadcast([128, 4, 64]), ALU.mult, ['ST', 'Pc'], ['ST'])
            pv = ps[7][:, :].rearrange('p (a x) -> p a x', x=128)
            tt('dve', ST[0:64], ST[0:64], pv[0:64, :, 0:64], ALU.add, ['ST', 'ps7'], ['ST'])
            tt('dve', ST[64:128], ST[64:128], pv[64:128, :, 64:128], ALU.add, ['ST', 'ps7'], ['ST'])
            cp('pool', STb, ST, ['ST'], ['STb'])
            if stp['last'] and si > 0:
                for hp in range(4):
                    tr(ps[7][0:64, hp * 128:(hp + 1) * 128], ST[:, hp, :], identf, ['ST', 'cst'], ['ps7'])
                cp('dve', flat3(stl), ps[7][0:64, :], ['ps7'], ['stl'])
                ld(ns_d[si - 1, d].rearrange('h v k -> v h k'), stl, [], ['stl'])
            if d == 1:
                return
            red(st3[:, 8:16], hv(ysum), ['ysum'], ['gn.s1'])
            tt('pool', sqy, ysum, ysum, ALU.mult, ['ysum'], ['sqy'])
            red(st3[:, 16:24], hv(sqy), ['sqy'], ['gn.s2'])
            ts('dve', st3[:, 24:32], st3[:, 8:16], 1.0 / 64, None, ALU.mult, None, ['gn.s1'], ['gn.mean'])
            tt('dve', st3[:, 32:40], st3[:, 24:32], st3[:, 24:32], ALU.mult, ['gn.mean'], ['gn.msq'])
            stt(st3[:, 40:48], st3[:, 16:24], 1.0 / 64, st3[:, 32:40], ALU.mult, ALU.subtract, ['gn.s2', 'gn.msq'], ['gn.ss'])
            rstd_of(st3[:, 40:48], st3[:, 48:56], st3[:, 56:64], 1.0, 64e-5, 'gn')
            b8 = lambda ap: ap.unsqueeze(2).to_broadcast([128, 8, 64])
            tt('pool', hv(ysum), hv(ysum), b8(st3[:, 24:32]), ALU.subtract, ['ysum', 'gn.mean'], ['ysum'])
            tt('pool', hv(ysum), hv(ysum), b8(st3[:, 56:64]), ALU.mult, ['ysum', 'gn.rstd'], ['ysum'])
            tt('pool', ysum, ysum, vecs_b[:, 3, :], ALU.mult, ['ysum', 'vecs'], ['ysum'])
            tt('pool', ysum, ysum, vecs_b[:, 4, :], ALU.add, ['ysum', 'vecs'], ['ysum'])
            tt('dve', bsf3[:, p3, :], bsf3[:, p3, :], bsb[:, t, :], ALU.add, ['bsf%d' % p3, 'bsb'], ['bsf%d' % p3])
            tt('pool', hv(sqy), hv(v_p[:, t, :]), b8(bsf3[:, p3, :]), ALU.mult, ['v_p', 'bsf%d' % p3, 'sqy'], ['sqy'])
            tt('pool', ysum, ysum, sqy, ALU.add, ['ysum', 'sqy'], ['ysum'])
            tt('pool', yrw[:, t, :], ysum, gsb, ALU.mult, ['ysum', 'gsb'], ['yrw'])

        steps = []
        for d in (1, 0):
            for si, (t0, ntile) in enumerate(SEQS):
                tiles = list(range(t0 + ntile - 1, t0 - 1, -1)) if d == 1 else list(range(t0, t0 + ntile))
                for k, t in enumerate(tiles):
                    steps.append(dict(t=t, d=d, si=si, first=(k == 0), last=(k == len(tiles) - 1)))
        NS = len(steps)
        gA = lambda i: prepA(steps[i]['t'], steps[i]['d'], i % 3) if i < NS else None
        for _ in gA(0):
            pass
        prepB(steps[0]['t'], steps[0]['d'], 0, gA(1))
        for i, stp in enumerate(steps):
            lc = P.record(lambda: chain(stp, i % 2, i % 3))
            ly = P.record(lambda: prepB(steps[i + 1]['t'], steps[i + 1]['d'], (i + 1) % 2, gA(i + 2))) if i + 1 < NS else []
            P.replay_merged([lc, ly])
        if stop_after <= 3:
            P.barrier()
            dbgf = A.view(PERS_END, [128, NT * 512], F32)
            cp('dve', dbgf, flat3(yrw), ['yrw'], ['dbgf'])
            dump('yrw', dbgf, ['dbgf'])
            return finish()

        P.barrier(exclude_groups=[outg])
        K1 = 1024
        delta1 = A.view(CONST_END, [128, NT, D], BF16)
        h2T = A.view(132 * K1, [128, 8, NTOK], BF16)
        s4 = Stack(A, 64 * K1, 130 * K1)
        Wout = s4.alloc([128, 8, D], BF16)
        mixT = [s4.alloc([128, 8, 128], BF16) for _ in range(2)]
        xt2 = [s4.alloc([128, D], F32) for _ in range(2)]
        x1 = s4.alloc([128, D], F32)
        junk4 = s4.alloc([128, D], F32)
        junk4c = s4.alloc([128, D], BF16)
        xs2 = [s4.alloc([128, D], BF16) for _ in range(2)]
        st4 = s4.alloc([128, 16], F32)
        g_wo = P.dma_group()
        ld(Wout[:, 0:4, :], wout_d[0:512, :].rearrange('(c p) n -> p c n', p=128), ['Wout'], group=g_wo, eng='pool')
        ld(Wout[0:64, 4:8, :], wout_d[512:768, :].rearrange('(c p) n -> p c n', p=64), ['Wout'], group=g_wo, eng='pool')
        ld(Wout[64:128, 4:8, :], wout_d[768:1024, :].rearrange('(c p) n -> p c n', p=64), ['Wout'], group=g_wo, eng='pool')
        def p4_stageA(t):
            b = t % 2
            row = 0 if t < 8 else 1
            for c in range(8):
                src = yrw[:, t, c * 128:(c + 1) * 128] if c < 4 else yat[:, t, (c - 4) * 128:(c - 3) * 128]
                tr(psb[0][:, c * 128:(c + 1) * 128], src, ident, ['yrw', 'yat', 'ident'], ['ps0'])
            cp('act', mixT[b].rearrange('p a b -> p (a b)'), psb[0][:, 0:1024], ['ps0'], ['mixT%d' % b])
            for blk in range(2):
                for c in range(8):
                    mm(ps[1 + blk][:, :], mixT[b][:, c, :], Wout[:, c, blk * 512:(blk + 1) * 512], c == 0, c == 7, ['mixT%d' % b, 'Wout'], ['ps%d' % (1 + blk)])
            act(junk4[:, 0:512], ps[1][:, :], AF.Square, ['ps1'], ['junk4a'], accum=st4[:, 0:1])
            act(junk4[:, 512:1024], ps[2][:, :], AF.Square, ['ps2'], ['junk4b'], accum=st4[:, 1:2])
            tt('dve', st4[:, 2:3], st4[:, 0:1], st4[:, 1:2], ALU.add, ['junk4a', 'junk4b'], ['m4.ss'])
            rstd_of(st4[:, 2:3], st4[:, 3:4], st4[:, 4:5], 1.0 / D, 1e-6, 'm4', use_pow=False)
            for blk in range(2):
                stt(delta1[:, t, blk * 512:(blk + 1) * 512], ps[1 + blk][:, :], st4[:, 4:5], G1b[:, row, blk * 512:(blk + 1) * 512],
                    ALU.mult, ALU.mult, ['ps%d' % (1 + blk), 'm4.rstd', 'G1b'], ['delta1'])
        def p4_stageB(t):
            b = t % 2
            row = 0 if t < 8 else 1
            ld(xt2[b], xs_d[t * 128:(t + 1) * 128, :], ['xt2%d' % b], pool='x')
            tt('dve', x1, xt2[b], delta1[:, t, :], ALU.add, ['xt2%d' % b, 'delta1'], ['x1'])
            act(junk4c, x1, AF.Square, ['x1'], ['junk4c', 'n2.ss'], accum=st4[:, 8:9])
            rstd_of(st4[:, 8:9], st4[:, 9:10], st4[:, 10:11], 1.0 / D, 1e-6, 'n2', use_pow=False)
            ts('dve', xs2[b], x1, st4[:, 10:11], None, ALU.mult, None, ['x1', 'n2.rstd'], ['xs2%d' % b])
        def p4_stageB2(t):
            b = t % 2
            row = 0 if t < 8 else 1
            for c in range(8):
                tr(psb[3][:, c * 128:(c + 1) * 128], xs2[b][:, c * 128:(c + 1) * 128], ident, ['xs2%d' % b, 'ident'], ['ps3'])
            for c in range(8):
                act(h2T[:, c, t * 128:(t + 1) * 128], psb[3][:, c * 128:(c + 1) * 128], AF.Identity, ['ps3', 'A2T', 'modT'], ['h2T'],
                    scale=A2T[:, c, row:row + 1], bias=B2T[:, c, row:row + 1])
        run_pipeline([p4_stageA, p4_stageB, p4_stageB2], NT)
        if stop_after <= 4:
            P.barrier()
            dbgf = A.view(64 * K1, [128, 8 * NTOK], F32)
            cp('dve', dbgf, h2T.rearrange('p a b -> p (a b)'), ['h2T'], ['dbgf'])
            dump('h2T', dbgf, ['dbgf'])
            return finish()

        P.barrier(exclude_groups=[outg])
        NPAD = 1542
        PADOFF = [1, 1027, 1285]
        ptok = lambda t: (1 + t * 128) if t < 8 else (1027 + (t - 8) * 128 if t < 10 else 1285 + (t - 10) * 128)
        actT = A.view(64 * K1, [128, 22, NPAD], BF16)
        s5 = Stack(A, 156 * K1, ARENA_BYTES)
        Wup = [s5.alloc([128, 8, 512], BF16) for _ in range(2)]
        ug = s5.alloc([128, NPAD], F32)
        uv = s5.alloc([128, NPAD], F32)
        cg = s5.alloc([128, NPAD], F32)
        cvv = s5.alloc([128, NPAD], F32)
        sgt = s5.alloc([128, NPAD], F32)
        fup_v = fup_d.rearrange('(c p) n -> p c n', p=128)
        P.op('pool', lambda e: e.memset(ug, 0.0), [], ['ug'])
        P.op('pool', lambda e: e.memset(uv, 0.0), [], ['uv'])

        def conv_chunk(u, dst, banks, jc, ku, kc):
            w0, w1, w2, bb_ = (convT[:, jc, i:i + 1] for i in range(4))
            cp('act', u[:, 1:513], ps[banks[0]][:, :], ['ps%d' % banks[0]], [ku])
            cp('act', u[:, 513:1025], ps[banks[1]][:, :], ['ps%d' % banks[1]], [ku])
            cp('act', u[:, 1027:1283], ps[banks[2]][:, 0:256], ['ps%d' % banks[2]], [ku])
            cp('act', u[:, 1285:1541], ps[banks[2]][:, 256:512], ['ps%d' % banks[2]], [ku])
            ts('pool', dst[:, 1:1541], u[:, 1:1541], w1, bb_, ALU.mult, ALU.add, [ku, 'convT'], [kc])
            stt(dst[:, 1:1541], u[:, 0:1540], w0, dst[:, 1:1541], ALU.mult, ALU.add, [ku, kc, 'convT'], [kc])
            stt(dst[:, 1:1541], u[:, 2:1542], w2, dst[:, 1:1541], ALU.mult, ALU.add, [ku, kc, 'convT'], [kc])

        def ld_piece(pi):
            g_u = P.dma_group('w')
            kw = 'Wup%d' % (pi % 2)
            ld(Wup[pi % 2][:, :, 0:256], fup_v[:, :, pi * 256:(pi + 1) * 256], [kw], group=g_u, eng='pool')
            ld(Wup[pi % 2][:, :, 256:512], fup_v[:, :, DFF + pi * 256:DFF + (pi + 1) * 256], [kw], group=g_u, eng='pool')

        ld_piece(0)
        for pi in range(11):
            wb = Wup[pi % 2]
            kw = 'Wup%d' % (pi % 2)
            for pp in range(2):
                j = pi * 2 + pp
                for half, (col0, banks) in enumerate(((pp * 128, (0, 1, 2)), (256 + pp * 128, (3, 4, 5)))):
                    for tg in range(3):
                        for c in range(8):
                            mm(ps[banks[tg]][:, :], wb[:, c, col0:col0 + 128], h2T[:, c, tg * 512:(tg + 1) * 512], c == 0, c == 7, [kw, 'h2T'], ['ps%d' % banks[tg]])
                if pp == 0 and pi + 1 < 11:
                    ld_piece(pi + 1)
                conv_chunk(ug, cg, (0, 1, 2), j, 'ug', 'cg')
                conv_chunk(uv, cvv, (3, 4, 5), 22 + j, 'uv', 'cvv')
                act(sgt[:, 1:1541], cg[:, 1:1541], AF.Silu, ['cg'], ['sgt'])
                tt('dve', actT[:, j, 1:1541], sgt[:, 1:1541], cvv[:, 1:1541], ALU.mult, ['sgt', 'cvv'], ['actT'])
        if stop_after <= 5:
            return finish()

        P.barrier(exclude_groups=[outg])
        Wdn = A.view(132 * K1, [128, 22, D], BF16)
        s6 = Stack(A, 176 * K1, ARENA_BYTES)
        xt3 = [s6.alloc([128, D], F32) for _ in range(2)]
        d2 = s6.alloc([128, D], F32)
        junk6 = s6.alloc([128, D], F32)
        yo = [s6.alloc([128, D], F32) for _ in range(2)]
        st6 = s6.alloc([128, 16], F32)
        g_wd = P.dma_group()
        fdn_v = fdn_d.rearrange('(j p) n -> p j n', p=128)
        WDK = []
        for qi, (j0, j1) in enumerate(((0, 6), (6, 11), (11, 17), (17, 22))):
            WDK.append((j0, j1, 'Wdn.%d' % qi))
            ld(Wdn[:, j0:j1, :], fdn_v[:, j0:j1, :], ['Wdn.%d' % qi], eng='pool')
        wdkey = lambda j: [k for (j0, j1, k) in WDK if j0 <= j < j1]
        for t in range(NT):
            b = t % 2
            row = 0 if t < 8 else 1
            pb = 2 * b
            for blk in range(2):
                for j in range(22):
                    mm(ps[pb + blk][:, :], actT[:, j, ptok(t):ptok(t) + 128], Wdn[:, j, blk * 512:(blk + 1) * 512], j == 0, j == 21, ['actT'] + wdkey(j), ['ps%d' % (pb + blk)])
            act(junk6[:, 0:512], ps[pb][:, :], AF.Square, ['ps%d' % pb], ['junk6a'], accum=st6[:, 0:1])
            act(junk6[:, 512:1024], ps[pb + 1][:, :], AF.Square, ['ps%d' % (pb + 1)], ['junk6b'], accum=st6[:, 1:2])
            tt('dve', st6[:, 2:3], st6[:, 0:1], st6[:, 1:2], ALU.add, ['junk6a', 'junk6b'], ['f6.ss'])
            rstd_of(st6[:, 2:3], st6[:, 3:4], st6[:, 4:5], 1.0 / D, 1e-6, 'f6', use_pow=False)
            for blk in range(2):
                stt(d2[:, blk * 512:(blk + 1) * 512], ps[pb + blk][:, :], st6[:, 4:5], G2b[:, row, blk * 512:(blk + 1) * 512],
                    ALU.mult, ALU.mult, ['ps%d' % (pb + blk), 'f6.rstd', 'G2b'], ['d2'])
            ld(xt3[b], xs_d[t * 128:(t + 1) * 128, :], ['xt3%d' % b], pool='x')
            tt('dve', yo[b], xt3[b], delta1[:, t, :], ALU.add, ['xt3%d' % b, 'delta1'], ['yo%d' % b])
            tt('pool', yo[b], yo[b], d2, ALU.add, ['yo%d' % b, 'd2'], ['yo%d' % b])
            ld(y_d[t * 128:(t + 1) * 128, :], yo[b], [], ['yo%d' % b], pool='o')
        P.barrier()
        P.emit(final_groups=[outg])
        return nc
        return finish()


def shard_inputs(inp, core):
    f = lambda a: np.ascontiguousarray(np.asarray(a, dtype=np.float32))
    m = {}
    m['xs'] = f(np.concatenate([inp['x_sample'][core], inp['x_prompt'][2 * core], inp['x_prompt'][2 * core + 1]], 0))
    m['cvec'] = f(np.stack([inp['c'][core], inp['c_ctx']], 0))
    m['ck'] = f(inp['cache_k'][core, 0].reshape(256, 128))
    m['cv'] = f(inp['cache_v'][core, 0].reshape(256, 128))
    m['st'] = f(inp['state_rwkv'][core, 0])
    m['w_mod'] = f(inp['w_mod'][0])
    m['b_mod'] = f(inp['b_mod'][0].reshape(1, -1))
    m['norms'] = f(np.stack([inp['norm_mix_pre'][0], inp['norm_ffn_pre'][0], inp['norm_mix_post'][0], inp['norm_ffn_post'][0]], 0))
    m['w_in'] = f(inp['w_in'][0])
    m['w0'] = f(inp['w0'][0].reshape(1, -1))
    m['w_up'] = f(inp['w_up'][0])
    m['a0'] = f(inp['a0'][0].reshape(1, -1))
    m['a_up'] = f(inp['a_up'][0])
    m['g_up'] = f(inp['g_up'][0])
    m['vecs'] = f(np.stack([inp['k_k'][0], inp['k_a'][0], inp['r_k'][0].reshape(-1), inp['gn_w'][0], inp['gn_b'][0]], 0))
    m['qkn'] = f(np.stack([inp['q_norm'][0], inp['k_norm'][0]], 0))
    m['w_out'] = f(inp['w_out'][0])
    m['ffn_up'] = f(inp['ffn_up'][0])
    m['convc'] = f(np.concatenate([inp['conv_w'][0], inp['conv_b'][0][None]], 0))
    m['ffn_down'] = f(inp['ffn_down'][0])
    m['cst'] = pack_consts()
    return m


def kernel(**inp):
    inp = {k: np.asarray(v) for k, v in inp.items()}
    nc = build()
    in_maps = [shard_inputs(inp, c) for c in range(8)]
    res = run_bass_kernel_spmd(nc, in_maps, core_ids=list(range(8)))
    R = res.results
    y_prompt = np.zeros((16, 256, 1024), np.float32)
    y_sample = np.zeros((8, 1024, 1024), np.float32)
    nk = np.zeros((16, 1, 256, 2, 64), np.float32)
    nv = np.zeros((16, 1, 256, 2, 64), np.float32)
    ns = np.zeros((16, 1, 2, 8, 64, 64), np.float32)
    for c in range(8):
        y = R[c]['y']
        y_sample[c] = y[0:1024]
        y_prompt[2 * c] = y[1024:1280]
        y_prompt[2 * c + 1] = y[1280:1536]
        nk[2 * c, 0] = R[c]['nk'][0:256].reshape(256, 2, 64)
        nk[2 * c + 1, 0] = R[c]['nk'][256:512].reshape(256, 2, 64)
        nv[2 * c, 0] = R[c]['nv'][0:256].reshape(256, 2, 64)
        nv[2 * c + 1, 0] = R[c]['nv'][256:512].reshape(256, 2, 64)
        ns[2 * c, 0] = R[c]['ns'][0]
        ns[2 * c + 1, 0] = R[c]['ns'][1]
    return (y_prompt, y_sample, nk, nv, ns)
```

```python
import numpy as np
from contextlib import ExitStack
import concourse.bass as bass
import concourse.mybir as mybir
from concourse.bass_utils import run_bass_kernel_spmd

F32 = mybir.dt.float32
BF16 = mybir.dt.bfloat16
AF = mybir.ActivationFunctionType
ALU = mybir.AluOpType
AX = mybir.AxisListType

NT = 12
NTOK = 1536
D = 1024
DFF = 2816
CDEC = -float(np.exp(-0.5))
SEQS = [(0, 8), (8, 2), (10, 2)]
KOFF = [0, 1280, 1536]
VT0 = [0, 10, 12]
NKT = [10, 2, 2]


class Prog:
    ENGS = ('pe', 'act', 'dve', 'pool', 'sp')

    def __init__(self, nc, self_sync=True):
        self.nc = nc
        self.ins = {e: [] for e in self.ENGS}
        self.lastw = {}
        self.readers = {}
        self.group_n = []
        self.group_pool = []
        self.self_sync = self_sync
        self.bar = set()
        self.capture = None
        self.closed = set()
        self.ps_last = {}
        self.bar_id = 0
        self.absorbed = {e: 0 for e in self.ENGS}

    def dma_group(self, pool=None):
        self.group_n.append(0)
        self.group_pool.append(pool)
        return len(self.group_n) - 1

    def _collect(self, reads, writes, eng=None):
        deps = set()
        for k in list(reads) + list(writes):
            if k.startswith('ps') and k[2:3].isdigit():
                for e2, tok in self.ps_last.get(k, {}).items():
                    if e2 != eng:
                        deps.add(tok)
        for k in reads:
            if k in self.lastw:
                deps.add(self.lastw[k])
        for k in writes:
            if k in self.lastw:
                deps.add(self.lastw[k])
            deps.update(self.readers.get(k, ()))
        return deps

    def _record(self, tok, reads, writes):
        if tok[0] == 'e':
            for k in list(reads) + list(writes):
                if k.startswith('ps') and k[2:3].isdigit():
                    self.ps_last.setdefault(k, {})[tok[1]] = tok
        for k in reads:
            self.readers.setdefault(k, []).append(tok)
        for k in writes:
            self.lastw[k] = tok
            self.readers[k] = []

    def op(self, eng, fn, reads=(), writes=()):
        if self.capture is not None:
            self.capture.append(('op', (eng, fn, tuple(reads), tuple(writes))))
            return
        deps = self._collect(reads, writes, eng) | self._bar_deps(eng)
        for d in deps:
            if d[0] == 'd':
                self.closed.add(d[1])
        idx = len(self.ins[eng])
        self.ins[eng].append(dict(fn=fn, deps=deps, group=None))
        self._record(('e', eng, idx), reads, writes)

    def dma(self, eng, fn, reads=(), writes=(), group=None, pool=None):
        if self.capture is not None:
            self.capture.append(('dma', (eng, fn, tuple(reads), tuple(writes), group, pool)))
            return None
        if group is None:
            group = self.dma_group(pool)
        deps = self._collect(reads, writes) | self._bar_deps(eng)
        deps.discard(('d', group))
        assert group not in self.closed, 'dma added to a group that already has waiters'
        for d in deps:
            if d[0] == 'd':
                self.closed.add(d[1])
        self.group_n[group] += 1
        self.ins[eng].append(dict(fn=fn, deps=deps, group=group))
        self._record(('d', group), reads, writes)
        return group

    def spacer(self, n):
        if self.capture is not None:
            self.capture.extend([('nop', None)] * n)

    def record(self, fn):
        assert self.capture is None
        self.capture = lst = []
        try:
            fn()
        finally:
            self.capture = None
        return lst

    def replay_merged(self, lists):
        lists = [l for l in lists if l]
        pos = [0] * len(lists)
        while True:
            best, bf = None, 2.0
            for i, l in enumerate(lists):
                if pos[i] < len(l):
                    f = pos[i] / len(l)
                    if f < bf:
                        best, bf = i, f
            if best is None:
                break
            kind, args = lists[best][pos[best]]
            pos[best] += 1
            if kind == 'op':
                self.op(*args)
            elif kind == 'dma':
                self.dma(*args)

    def barrier(self, exclude_groups=()):
        toks = set()
        for e in self.ENGS:
            for i in range(len(self.ins[e]) - 1, -1, -1):
                if self.ins[e][i]['group'] is None:
                    toks.add(('e', e, i))
                    break
        for g in range(len(self.group_n)):
            if self.group_n[g] > 0 and g not in exclude_groups:
                toks.add(('d', g))
        self.bar = toks
        self.bar_id += 1
        self.lastw = {}
        self.readers = {}
        self.ps_last = {}

    def _bar_deps(self, eng):
        if self.absorbed[eng] < self.bar_id:
            self.absorbed[eng] = self.bar_id
            return set(self.bar)
        return set()

    def emit(self, final_groups=()):
        nc = self.nc
        needed = {e: set() for e in self.ENGS}
        for e in self.ENGS:
            for ins in self.ins[e]:
                for d in ins['deps']:
                    if d is not None and d[0] == 'e':
                        if d[1] == e and (e == 'pe' or not self.self_sync):
                            continue
                        needed[d[1]].add(d[2])
        cnt = {}
        for e in self.ENGS:
            c = 0
            arr = []
            for i in range(len(self.ins[e])):
                if i in needed[e]:
                    c += 1
                arr.append(c)
            cnt[e] = arr
        with ExitStack() as st:
            esem = {e: st.enter_context(nc.semaphore('s_' + e)) for e in self.ENGS}
            POOLK = 4
            pools = {}
            gsem, gtarget = [], []
            for g in range(len(self.group_n)):
                pl = self.group_pool[g]
                if pl is None:
                    gsem.append(st.enter_context(nc.semaphore('g%d' % g)))
                    gtarget.append(16 * self.group_n[g])
                else:
                    if pl not in pools:
                        pools[pl] = dict(sems=[st.enter_context(nc.semaphore('p%s%d' % (pl, i))) for i in range(POOLK)],
                                         cnt=[0] * POOLK, nxt=0)
                    pd = pools[pl]
                    i = pd['nxt'] % POOLK
                    pd['nxt'] += 1
                    pd['cnt'][i] += 16 * self.group_n[g]
                    gsem.append(pd['sems'][i])
                    gtarget.append(pd['cnt'][i])
            block = st.enter_context(nc.Block())

            def run(e, engobj):
                seen = {x: -1 for x in self.ENGS}
                seeng = set()
                for i, ins in enumerate(self.ins[e]):
                    for d in sorted((x for x in ins['deps'] if x is not None), key=str):
                        if d[0] == 'e':
                            _, e2, i2 = d
                            if e2 == e and (e == 'pe' or not self.self_sync):
                                continue
                            if i2 > seen[e2]:
                                engobj.wait_ge(esem[e2], cnt[e2][i2])
                                seen[e2] = i2
                        else:
                            g = d[1]
                            if g not in seeng:
                                engobj.wait_ge(gsem[g], gtarget[g])
                                seeng.add(g)
                    r = ins['fn'](engobj)
                    if ins['group'] is not None:
                        r.then_inc(gsem[ins['group']], 16)
                    elif i in needed[e]:
                        r.then_inc(esem[e], 1)
                mine = [g for g in range(len(self.group_n)) if self.group_n[g] > 0]
                for g in mine:
                    engobj.wait_ge(gsem[g], gtarget[g])
                for e2 in self.ENGS:
                    if cnt[e2] and cnt[e2][-1] > 0:
                        engobj.wait_ge(esem[e2], cnt[e2][-1])

            @block.tensor
            def _(eng):
                run('pe', eng)

            @block.scalar
            def _(eng):
                run('act', eng)

            @block.vector
            def _(eng):
                run('dve', eng)

            @block.gpsimd
            def _(eng):
                run('pool', eng)

            @block.sync
            def _(eng):
                run('sp', eng)


class Arena:
    def __init__(self, base_ap, nbytes):
        self.base = base_ap
        self.nbytes = nbytes

    def view(self, off, shape, dt):
        esz = 2 if dt == BF16 else 4
        n = int(np.prod(shape[1:]))
        nb = n * esz
        assert off % 4 == 0 and off + nb <= self.nbytes, (off, nb, self.nbytes)
        nb4 = (nb + 3) // 4
        ap = self.base[:, off // 4: off // 4 + nb4]
        if dt == BF16:
            ap = ap.bitcast(BF16)[:, 0:n]
        if len(shape) > 2:
            names = ' '.join('d%d' % i for i in range(1, len(shape)))
            kw = {'d%d' % i: shape[i] for i in range(1, len(shape))}
            ap = ap.rearrange('p (%s) -> p %s' % (names, names), **kw)
        if shape[0] < 128:
            ap = ap[0:shape[0]]
        return ap


class Stack:
    def __init__(self, arena, lo, hi):
        self.a, self.lo, self.hi, self.cur = arena, lo, hi, lo

    def alloc(self, shape, dt):
        esz = 2 if dt == BF16 else 4
        nb = int(np.prod(shape[1:])) * esz
        nb = (nb + 31) // 32 * 32
        off = self.cur
        assert off + nb <= self.hi, ('arena overflow', off, nb, self.hi)
        self.cur += nb
        return self.a.view(off, shape, dt)


def host_consts():
    idx = np.arange(128)
    incl_f = (idx[:, None] <= idx[None, :]).astype(np.float32)
    strict_f = (idx[:, None] < idx[None, :]).astype(np.float32)
    incl_b = (idx[:, None] >= idx[None, :]).astype(np.float32)
    strict_b = (idx[:, None] > idx[None, :]).astype(np.float32)
    c = {}
    c['ident'] = np.eye(128, dtype=np.float32)
    c['tri'] = np.stack([incl_f, strict_f, strict_b, incl_b, strict_b, strict_f], 1)
    mm_f = np.concatenate([strict_f, incl_f, strict_f, incl_f], 1)
    mm_b = np.concatenate([strict_b, incl_b, strict_b, incl_b], 1)
    c['mmask'] = np.stack([mm_f, mm_b], 1)
    qt_f = np.concatenate([strict_f.T] * 4, 1)
    qt_b = np.concatenate([strict_b.T] * 4, 1)
    c['qtmask'] = np.stack([qt_f, qt_b], 1)
    oh = np.zeros((128, 2), np.float32)
    oh[127, 0] = 1.0
    oh[0, 1] = 1.0
    c['onehot'] = oh
    pidx = np.arange(128, dtype=np.float32)
    cnt = {0: (pidx + 1, pidx, 127 - pidx), 1: (128 - pidx, 127 - pidx, pidx)}
    cb = np.zeros((128, 2, 4), np.float32)
    for d_ in (0, 1):
        ci, cs_, cr = cnt[d_]
        cb[:, d_, 0] = 0.5 * CDEC * ci
        cb[:, d_, 1] = -0.5 * CDEC * ci
        cb[:, d_, 2] = 0.5 * CDEC * cs_
        cb[:, d_, 3] = 0.5 * CDEC * cr
    c['cbias'] = cb
    sel = np.zeros((128, 2, 128), np.float32)
    sel[0, 0, :] = 1.0
    sel[1, 1, :] = 1.0
    c['sel'] = sel
    t = np.arange(1024)
    quarter = 16
    inv = (10000.0 ** (-np.arange(quarter, dtype=np.float32) / quarter)).astype(np.float32)
    ang_r = (t // 64).astype(np.float32)[:, None] * inv[None, :]
    ang_c = (t % 64).astype(np.float32)[:, None] * inv[None, :]
    cos = np.stack([np.cos(ang_r), np.cos(ang_c)], 1).astype(np.float32)
    sin = np.stack([np.sin(ang_r), np.sin(ang_c)], 1).astype(np.float32)
    c['cos'] = cos.reshape(8, 128, 32).transpose(1, 0, 2).copy()
    c['sin'] = sin.reshape(8, 128, 32).transpose(1, 0, 2).copy()
    return c


CST_LAYOUT = [('ident', 128), ('tri', 768), ('onehot', 2), ('cbias', 8),
              ('sel', 256), ('cos', 256), ('sin', 256), ('mmask', 1024), ('qtmask', 1024)]
CST_COLS = sum(n for _, n in CST_LAYOUT)
CST_KEEP = CST_COLS - 2048


def pack_consts():
    c = host_consts()
    out = np.zeros((128, CST_COLS), np.float32)
    o = 0
    for name, n in CST_LAYOUT:
        out[:, o:o + n] = c[name].reshape(128, n)
        o += n
    return out


def build(stop_after=99, dbg=None):
    nc = bass.Bass("TRN2", target_bir_lowering=False)
    dt_in = lambda name, shape: nc.dram_tensor(name, shape, F32, kind="ExternalInput").ap()
    dt_out = lambda name, shape: nc.dram_tensor(name, shape, F32, kind="ExternalOutput").ap()
    xs_d = dt_in("xs", [NTOK, D])
    cvec_d = dt_in("cvec", [2, D])
    ck_d = dt_in("ck", [256, 128])
    cv_d = dt_in("cv", [256, 128])
    st_d = dt_in("st", [2, 8, 64, 64])
    wmod_d = dt_in("w_mod", [D, 6 * D])
    bmod_d = dt_in("b_mod", [1, 6 * D])
    nrm_d = dt_in("norms", [4, D])
    win_d = dt_in("w_in", [D, 2560])
    w0_d = dt_in("w0", [1, 1024])
    wup_d = dt_in("w_up", [2, 64, 512])
    a0_d = dt_in("a0", [1, 1024])
    aup_d = dt_in("a_up", [2, 64, 512])
    gup_d = dt_in("g_up", [128, 512])
    vecs_d = dt_in("vecs", [5, 512])
    qkn_d = dt_in("qkn", [2, 64])
    wout_d = dt_in("w_out", [D, D])
    fup_d = dt_in("ffn_up", [D, 2 * DFF])
    convc_d = dt_in("convc", [4, 2 * DFF])
    fdn_d = dt_in("ffn_down", [DFF, D])
    cst_d = dt_in("cst", [128, CST_COLS])
    y_d = dt_out("y", [NTOK, D])
    nk_d = dt_out("nk", [512, 128])
    nv_d = dt_out("nv", [512, 128])
    ns_d = dt_out("ns", [2, 2, 8, 64, 64])
    dbg_d = {}
    if dbg:
        for name, shape in dbg.items():
            dbg_d[name] = dt_out("dbg_" + name, list(shape))

    ARENA_BYTES = 207 * 1024
    with ExitStack() as st:
        arena_t = st.enter_context(nc.sbuf_tensor("arena", [128, ARENA_BYTES // 4], F32))
        ps = [st.enter_context(nc.psum_tensor("ps%d" % i, [128, 512], F32)) for i in range(8)]
        psb = [t[:].bitcast(BF16) for t in ps]
        A = Arena(arena_t[:], ARENA_BYTES)
        P = Prog(nc)
        outg = P.dma_group()

        def mm(out, lhsT, rhs, start, stop, r, w, skip=False):
            P.op('pe', lambda e: e.matmul(out, lhsT=lhsT, rhs=rhs, start=start, stop=stop, skip_group_check=skip), r, w)

        def tr(out, in_, idn, r, w):
            P.op('pe', lambda e: e.transpose(out=out, in_=in_, identity=idn), r, w)

        def act(out, in_, func, r, w, scale=None, bias=None, accum=None, eng='act'):
            kw = {}
            if scale is not None:
                kw['scale'] = scale
            if bias is not None:
                kw['bias'] = bias
            if accum is not None:
                kw['accum_out'] = accum
            P.op('act', lambda e: e.activation(out=out, in_=in_, func=func, **kw), r, w)

        def tt(eng, out, in0, in1, op, r, w):
            P.op(eng, lambda e: e.tensor_tensor(out=out, in0=in0, in1=in1, op=op), r, w)

        def ts(eng, out, in0, s1, s2, op0, op1, r, w):
            if op1 is None:
                P.op(eng, lambda e: e.tensor_scalar(out=out, in0=in0, scalar1=s1, scalar2=None, op0=op0), r, w)
            else:
                P.op(eng, lambda e: e.tensor_scalar(out=out, in0=in0, scalar1=s1, scalar2=s2, op0=op0, op1=op1), r, w)

        def stt(out, in0, scalar, in1, op0, op1, r, w):
            P.op('dve', lambda e: e.scalar_tensor_tensor(out=out, in0=in0, scalar=scalar, in1=in1, op0=op0, op1=op1), r, w)

        def cp(eng, out, in_, r, w):
            if eng == 'act':
                P.op('act', lambda e: e.activation(out=out, in_=in_, func=AF.Identity), r, w)
            else:
                P.op(eng, lambda e: e.tensor_copy(out=out, in_=in_), r, w)

        def red(out, in_, r, w, op=ALU.add):
            P.op('dve', lambda e: e.tensor_reduce(out=out, in_=in_, axis=AX.X, op=op), r, w)

        def ld(out, in_, w, r=(), group=None, eng='sp', pool=None, **kw):
            return P.dma(eng, lambda e: e.dma_start(out=out, in_=in_, **kw), r, w, group, pool)

        def rstd_of(ss, tmp, out, inv_n, eps, key, use_pow=True):
            ts('dve', tmp, ss, inv_n, eps, ALU.mult, ALU.add, [key + '.ss'], [key + '.tmp'])
            if not use_pow:
                act(tmp, tmp, AF.Sqrt, [key + '.tmp'], [key + '.tmp'])
                P.op('dve', lambda e: e.reciprocal(out=out, in_=tmp), [key + '.tmp'], [key + '.rstd'])
                return
            n = tmp.shape[-1]
            tt('pool', out, tmp, mhalf[:, 0:n], ALU.pow, [key + '.tmp', 'mhalf'], [key + '.rstd'])

        def dump(name, ap, keys):
            if name in dbg_d:
                ld(dbg_d[name], ap, [], keys)

        def finish():
            P.barrier()
            fin = A.view(PERS_END, [128, 64], F32)
            P.op('dve', lambda e: e.memset(fin, 0.0), [], ['fin'])
            ld(y_d[0:128, 0:64], fin, [], ['fin'])
            P.emit(final_groups=[outg])
            return nc

        CONST_END = 40 * 1024
        PERS_END = CONST_END + 76 * 1024
        YY_OFF = 183 * 1024
        sc = Stack(A, 0, CONST_END)
        cst = sc.alloc([128, CST_KEEP], F32)
        cstm = A.view(ARENA_BYTES - 8192, [128, 2048], F32)
        o = 0
        cv = {}
        for name, n in CST_LAYOUT:
            cv[name] = cst[:, o:o + n] if o < CST_KEEP else cstm[:, o - CST_KEEP:o - CST_KEEP + n]
            o += n
        identf = cv['ident']
        trif = cv['tri'].rearrange('p (a b) -> p a b', a=6)
        onehot = cv['onehot']
        cbias = cv['cbias'].rearrange('p (a b) -> p a b', a=2)
        sel = cv['sel'].rearrange('p (a b) -> p a b', a=2)
        cosf = cv['cos'].rearrange('p (t r f) -> p t r f', t=8, r=2)
        sinf = cv['sin'].rearrange('p (t r f) -> p t r f', t=8, r=2)
        ident = sc.alloc([128, 128], BF16)
        mmask = sc.alloc([128, 2, 512], BF16)
        qtmask = sc.alloc([128, 2, 512], BF16)
        ones_bf = sc.alloc([128, 128], BF16)
        tri = sc.alloc([128, 6, 128], BF16)
        modT = sc.alloc([128, 4, 8, 2], F32)
        gT = sc.alloc([128, 8, 2], F32)
        A1T = sc.alloc([128, 8, 2], F32)
        A2T = sc.alloc([128, 8, 2], F32)
        G1b = sc.alloc([128, 2, 1024], BF16)
        G2b = sc.alloc([128, 2, 1024], BF16)
        vecs_b = sc.alloc([128, 5, 512], F32)
        qkn_b = sc.alloc([128, 2, 64], F32)
        convT = sc.alloc([128, 44, 4], F32)
        lora_w = sc.alloc([128, 2, 512], BF16)
        w0a0 = sc.alloc([128, 1024], BF16)
        gup_bf = sc.alloc([128, 512], BF16)
        mhalf = sc.alloc([128, 8], F32)
        silucT = sc.alloc([128, 8, 2], BF16)

        ld(cst, cst_d[:, 0:CST_KEEP], ['cst'])
        ld(cstm, cst_d[:, CST_KEEP:CST_COLS], ['cstm'])
        cp('dve', ident, identf, ['cst'], ['ident'])
        cp('dve', mmask.rearrange('p a b -> p (a b)'), cv['mmask'], ['cstm'], ['mmask'])
        cp('dve', qtmask.rearrange('p a b -> p (a b)'), cv['qtmask'], ['cstm'], ['qtmask'])
        P.op('dve', lambda e: e.memset(ones_bf, 1.0), [], ['ones'])
        P.op('pool', lambda e: e.memset(mhalf, -0.5), [], ['mhalf'])
        cp('dve', tri, trif, ['cst'], ['tri'])
        g_c = P.dma_group()
        ld(vecs_b.rearrange('p a b -> p (a b)'), vecs_d.rearrange('a b -> (a b)').partition_broadcast(128), ['vecs'], group=g_c)
        ld(qkn_b.rearrange('p a b -> p (a b)'), qkn_d.rearrange('a b -> (a b)').partition_broadcast(128), ['qkn'], group=g_c)
        g_l = P.dma_group()
        ld(lora_w[0:64], wup_d.rearrange('d k n -> k d n'), ['lora'], group=g_l, eng='pool')
        ld(lora_w[64:128], aup_d.rearrange('d k n -> k d n'), ['lora'], group=g_l, eng='pool')
        ld(w0a0[0:1, :], w0_d, ['lora'], group=g_l, eng='pool')
        ld(w0a0[64:65, :], a0_d, ['lora'], group=g_l, eng='pool')
        ld(gup_bf, gup_d, ['lora'], group=g_l, eng='pool')

        if stop_after <= -3:
            P.emit(final_groups=[outg])
            return nc
        sp_ = Stack(A, CONST_END, PERS_END)
        r_p = sp_.alloc([128, NT, 512], BF16)
        v_p = sp_.alloc([128, NT, 512], BF16)
        k_p = sp_.alloc([128, NT, 512], BF16)
        kkn_p = sp_.alloc([128, NT, 512], BF16)
        txT = sp_.alloc([128, NTOK], BF16)
        sgT = sp_.alloc([128, NTOK], BF16)
        qT = sp_.alloc([128, 4, NTOK], BF16)
        kT_all = sp_.alloc([128, 1792], BF16)
        Vaug = sp_.alloc([128, 14, 2, 66], BF16)

        s0 = Stack(A, CONST_END, ARENA_BYTES - 8192)
        m_sb = s0.alloc([2, 6 * D], F32)
        c_sb = s0.alloc([2, D], F32)
        silu_c = s0.alloc([2, D], BF16)
        bmod_bf = s0.alloc([1, 6 * D], BF16)
        nrm_sb = s0.alloc([4, D], F32)
        gpost_b = s0.alloc([128, 2, D], F32)
        wm = [s0.alloc([128, 8, 512], BF16) for _ in range(2)]
        convrows = s0.alloc([4, 2 * DFF], F32)

        ld(c_sb, cvec_d, ['c_sb'])
        ld(bmod_bf, bmod_d, ['bmod'], eng='pool', max_dma_last_dim=2048)
        ld(nrm_sb, nrm_d, ['nrm'])
        ld(gpost_b[:, 0, :], nrm_d[2].partition_broadcast(128), ['gpost0'])
        ld(gpost_b[:, 1, :], nrm_d[3].partition_broadcast(128), ['gpost1'])
        act(silu_c, c_sb, AF.Silu, ['c_sb'], ['silu_c'])
        for c in range(8):
            tr(psb[0][:, c * 2:(c + 1) * 2], silu_c[0:2, c * 128:(c + 1) * 128], ident[0:2, 0:2], ['silu_c', 'ident'], ['ps0'])
        cp('dve', silucT.rearrange('p a b -> p (a b)'), psb[0][:, 0:16], ['ps0'], ['silucT'])
        if stop_after <= -2:
            P.emit(final_groups=[outg])
            return nc
        wmod_v = wmod_d.rearrange('(c p) n -> p c n', p=128)
        for j in range(4):
            b = wm[j % 2]
            kb = 'wm%d' % (j % 2)
            ld(b, wmod_v[:, :, j * 512:(j + 1) * 512], [kb], eng='pool', pool='w')
            pk = 'ps%d' % (1 + j % 2)
            pt = ps[1 + j % 2]
            for c in range(8):
                mm(pt[0:2, :], silucT[:, c, :], b[:, c, :], c == 0, False, ['silucT', kb], [pk])
            mm(pt[0:2, :], ones_bf[0:1, 0:2], bmod_bf[0:1, j * 512:(j + 1) * 512], False, True, ['ones', 'bmod'], [pk])
            cp('act' if j % 2 else 'dve', m_sb[:, j * 512:(j + 1) * 512], pt[0:2, :], [pk], ['m_sb'])
        if stop_after <= -1:
            dump('m', m_sb, ['m_sb'])
            P.emit(final_groups=[outg])
            return nc
        for ki, kind in enumerate((0, 1)):
            for c in range(8):
                i2 = (ki * 8 + c) * 2
                tr(ps[3][:, i2:i2 + 2], m_sb[0:2, kind * D + c * 128: kind * D + (c + 1) * 128], identf[0:2, 0:2], ['m_sb', 'cst'], ['ps3'])
        for c in range(8):
            i2 = 64 + c * 2
            tr(ps[3][:, i2:i2 + 2], nrm_sb[0:2, c * 128:(c + 1) * 128], identf[0:2, 0:2], ['nrm', 'cst'], ['ps3'])
        cp('dve', modT[:, 0:2].rearrange('p a b c -> p (a b c)'), ps[3][:, 0:32], ['ps3'], ['modT01'])
        cp('dve', gT.rearrange('p a b -> p (a b)'), ps[3][:, 64:80], ['ps3'], ['gT'])
        stt(A1T, modT[:, 1], 1.0, gT[:, :, 0:1].to_broadcast([128, 8, 2]), ALU.add, ALU.mult, ['modT01', 'gT'], ['A1T'])
        B1T = modT[:, 0]
        B2T = modT[:, 2]
        ld(convrows, convc_d, ['convrows'])
        for j in range(44):
            tr(ps[5][:, j * 4:(j + 1) * 4], convrows[0:4, j * 128:(j + 1) * 128], identf[0:4, 0:4], ['convrows', 'cst'], ['ps5'])
        cp('dve', convT.rearrange('p a b -> p (a b)'), ps[5][:, 0:176], ['ps5'], ['convT'])
        if dbg and 'm' in dbg:
            dump('m', m_sb, ['m_sb'])
        if dbg and 'A1T' in dbg:
            dump('A1T', A1T, ['A1T'])
        if stop_after <= 0:
            P.emit(final_groups=[outg])
            return nc

        P.barrier(exclude_groups=[outg])
        s1 = Stack(A, PERS_END, ARENA_BYTES)
        Win = s1.alloc([128, 8, 2560], BF16)
        xt = [s1.alloc([128, D], F32) for _ in range(2)]
        xsb = [s1.alloc([128, D], BF16) for _ in range(2)]
        hT = [s1.alloc([128, 8, 128], BF16) for _ in range(2)]
        junk = s1.alloc([128, D], F32)
        kkf = s1.alloc([128, 512], F32)
        sq2 = s1.alloc([128, 512], F32)
        sq3 = s1.alloc([128, 512], F32)
        qf = s1.alloc([128, 512], F32)
        qb = s1.alloc([128, 512], BF16)
        ta = s1.alloc([128, 256], F32)
        tb = s1.alloc([128, 256], F32)
        lo = s1.alloc([128, 256], BF16)
        kaf = s1.alloc([128, 128], F32)
        kab = s1.alloc([128, 128], BF16)
        vaf = s1.alloc([128, 128], F32)
        ckf = s1.alloc([128, 2, 128], F32)
        ckb = s1.alloc([128, 2, 128], BF16)
        stat = s1.alloc([128, 64], F32)
        qs = [s1.alloc([128, 512], F32) for _ in range(2)]
        ms = [s1.alloc([128, 512], F32) for _ in range(2)]

        win_v = win_d.rearrange('(c p) n -> p c n', p=128)
        dq = Win[:, :, 1536:2048].rearrange('p c (hh g d) -> p c hh g d', g=2, d=64)
        wl = [(Win[:, :, b3 * 512:(b3 + 1) * 512], win_v[:, :, b3 * 512:(b3 + 1) * 512]) for b3 in range(3)]
        for g in range(2):
            for hh in range(4):
                c0 = 1792 + (g * 4 + hh) * 64
                wl.append((dq[:, :, hh, g, :], win_v[:, :, c0:c0 + 64]))
        wl.append((Win[:, :, 2048:2304], win_v[:, :, 1536:1792]))
        wl.append((Win[:, :, 2304:2560], win_v[:, :, 2304:2560]))
        WINK = ['Win.%d' % i for i in range(len(wl))]
        for i, (dst, src) in enumerate(wl):
            ld(dst, src, [WINK[i]], [WINK[i - 4]] if i >= 4 else [], eng='pool', pool='w')
        P.op('pool', lambda e: e.memset(Vaug.rearrange('p a b c -> p (a b c)'), 1.0), [], ['Vaug'])
        ld(ckf, ck_d.rearrange('(a p) n -> p a n', p=128), ['ckf'])
        g_cv = P.dma_group()
        for a in range(2):
            ld(Vaug[:, 8 + a, :, 0:64], cv_d[a * 128:(a + 1) * 128, :].rearrange('p (g d) -> p g d', g=2), ['Vaug'], eng='pool', group=g_cv)
        cp('dve', ckb, ckf, ['ckf'], ['ckb'])
        for a in range(2):
            tr(psb[7][:, a * 128:(a + 1) * 128], ckb[:, a, :], ident, ['ckb', 'ident'], ['ps7'])
        cp('dve', kT_all[:, 1024:1280], psb[7][:, 0:256], ['ps7'], ['kT'])

        def rope(src, dst, nh, t, key):
            n = nh * 2 * 16
            sv = src.rearrange('p (h r x f) -> p h r x f', r=2, x=2, f=16)
            dv = dst.rearrange('p (h r x f) -> p h r x f', r=2, x=2, f=16)
            x1, x2 = sv[:, :, :, 0, :], sv[:, :, :, 1, :]
            cs = cosf[:, t].unsqueeze(1).to_broadcast([128, nh, 2, 16])
            sn = sinf[:, t].unsqueeze(1).to_broadcast([128, nh, 2, 16])
            tav = ta[:, 0:n].rearrange('p (h r f) -> p h r f', r=2, f=16)
            tbv = tb[:, 0:n].rearrange('p (h r f) -> p h r f', r=2, f=16)
            tt('pool', tav, x1, cs, ALU.mult, [key], ['ta'])
            tt('dve', tbv, x2, sn, ALU.mult, [key], ['tb'])
            tt('pool', dv[:, :, :, 0, :], tav, tbv, ALU.subtract, ['ta', 'tb'], [key + 'b'])
            tt('pool', tav, x1, sn, ALU.mult, [key], ['ta'])
            tt('dve', tbv, x2, cs, ALU.mult, [key], ['tb'])
            tt('pool', dv[:, :, :, 1, :], tav, tbv, ALU.add, ['ta', 'tb'], [key + 'b'])

        def p1_stageA(t):
            b = t % 2
            row = 0 if t < 8 else 1
            kx, ks, kh = 'xt%d' % b, 'xs%d' % b, 'hT%d' % b
            ld(xt[b], xs_d[t * 128:(t + 1) * 128, :], [kx], pool='x')
            act(junk, xt[b], AF.Square, [kx], ['junk', 'n1.ss'], accum=stat[:, 0:1])
            rstd_of(stat[:, 0:1], stat[:, 1:2], stat[:, 2:3], 1.0 / D, 1e-6, 'n1')
            ts('dve', xsb[b], xt[b], stat[:, 2:3], None, ALU.mult, None, [kx, 'n1.rstd'], [ks])
            for c in range(8):
                tr(psb[0][:, c * 128:(c + 1) * 128], xsb[b][:, c * 128:(c + 1) * 128], ident, [ks, 'ident'], ['ps0'])
            for c in range(8):
                act(hT[b][:, c, :], psb[0][:, c * 128:(c + 1) * 128], AF.Identity, ['ps0', 'A1T', 'modT01'], [kh],
                    scale=A1T[:, c, row:row + 1], bias=B1T[:, c, row:row + 1])
        def p1_stageA2(t):
            b = t % 2
            kh = 'hT%d' % b
            for blk in range(5):
                for c in range(8):
                    mm(ps[1 + blk][:, :], hT[b][:, c, :], Win[:, c, blk * 512:(blk + 1) * 512], c == 0, c == 7, [kh] + WINK, ['ps%d' % (1 + blk)])
            kq, km = 'qs%d' % b, 'ms%d' % b
            cp('act', r_p[:, t, :], ps[1][:, :], ['ps1'], ['r_p'])
            cp('act', k_p[:, t, :], ps[2][:, :], ['ps2'], ['k_p'])
            cp('act', v_p[:, t, :], ps[3][:, :], ['ps3'], ['v_p'])
            cp('act', qs[b], ps[4][:, :], ['ps4'], [kq])
            cp('act', ms[b], ps[5][:, :], ['ps5'], [km])
        def p1_stageB(t):
            b = t % 2
            row = 0 if t < 8 else 1
            kq, km = 'qs%d' % b, 'ms%d' % b
            tt('dve', kkf, k_p[:, t, :], vecs_b[:, 0, :], ALU.mult, ['k_p', 'vecs'], ['kkf'])
            tt('dve', sq2, kkf, kkf, ALU.mult, ['kkf'], ['sq2'])
            red(stat[:, 8:16], sq2.rearrange('p (h d) -> p h d', d=64), ['sq2'], ['nk.ss'])
            rstd_of(stat[:, 8:16], stat[:, 16:24], stat[:, 24:32], 1.0, 1e-12, 'nk')
            tt('pool', kkn_p[:, t, :].rearrange('p (h d) -> p h d', d=64), kkf.rearrange('p (h d) -> p h d', d=64),
               stat[:, 24:32].unsqueeze(2).to_broadcast([128, 8, 64]), ALU.mult, ['kkf', 'nk.rstd'], ['kkn_p'])
            tt('dve', sq3, qs[b], qs[b], ALU.mult, [kq], ['sq3'])
            red(stat[:, 32:40], sq3.rearrange('p (h d) -> p h d', d=64), ['sq3'], ['nq.ss'])
            rstd_of(stat[:, 32:40], stat[:, 40:48], stat[:, 48:56], 1.0 / 64, 1e-6, 'nq')
            tt('dve', qf.rearrange('p (h d) -> p h d', d=64), qs[b].rearrange('p (h d) -> p h d', d=64),
               stat[:, 48:56].unsqueeze(2).to_broadcast([128, 8, 64]), ALU.mult, [kq, 'nq.rstd'], ['qf'])
            tt('pool', qf.rearrange('p (h d) -> p h d', d=64), qf.rearrange('p (h d) -> p h d', d=64),
               qkn_b[:, 0:1, :].to_broadcast([128, 8, 64]), ALU.mult, ['qf', 'qkn'], ['qf'])
            if t < 8:
                rope(qf, qb, 8, t, 'qf')
            else:
                cp('pool', qb, qf, ['qf'], ['qfb'])
            for hp in range(4):
                tr(psb[6][:, hp * 128:(hp + 1) * 128], qb[:, hp * 128:(hp + 1) * 128], ident, ['qfb', 'ident'], ['ps6'])
            cp('dve', qT[:, :, t * 128:(t + 1) * 128], psb[6][:, 0:512].rearrange('p (a b) -> p a b', b=128), ['ps6'], ['qT'])
            act(lo[:, 0:64], ms[b][:, 0:64], AF.Tanh, [km], ['lo'])
            cp('dve', lo[:, 64:128], ms[b][:, 64:128], [km], ['lo'])
            act(lo[:, 128:256], ms[b][:, 128:256], AF.Sigmoid, [km], ['lo'])
            tr(psb[7][:, 0:128], lo[:, 0:128], ident, ['lo', 'ident'], ['ps7'])
            tr(psb[7][:, 128:256], lo[:, 128:256], ident, ['lo', 'ident'], ['ps7'])
            cp('dve', txT[:, t * 128:(t + 1) * 128], psb[7][:, 0:128], ['ps7'], ['txT'])
            cp('dve', sgT[:, t * 128:(t + 1) * 128], psb[7][:, 128:256], ['ps7'], ['sgT'])
            tt('pool', sq3[:, 0:128], ms[b][:, 256:384], ms[b][:, 256:384], ALU.mult, [km, 'sq3'], ['sq3'])
            red(stat[:, 56:58], sq3[:, 0:128].rearrange('p (h d) -> p h d', d=64), ['sq3'], ['na.ss'])
            rstd_of(stat[:, 56:58], stat[:, 58:60], stat[:, 60:62], 1.0 / 64, 1e-6, 'na')
            tt('dve', kaf.rearrange('p (h d) -> p h d', d=64), ms[b][:, 256:384].rearrange('p (h d) -> p h d', d=64),
               stat[:, 60:62].unsqueeze(2).to_broadcast([128, 2, 64]), ALU.mult, [km, 'na.rstd'], ['kaf'])
            tt('pool', kaf.rearrange('p (h d) -> p h d', d=64), kaf.rearrange('p (h d) -> p h d', d=64),
               qkn_b[:, 1:2, :].to_broadcast([128, 2, 64]), ALU.mult, ['kaf', 'qkn'], ['kaf'])
            if t < 8:
                rope(kaf, kab, 2, t, 'kaf')
                kcol = t * 128
                vt = t
            else:
                cp('pool', kab, kaf, ['kaf'], ['kafb'])
                sq_i = 1 if t < 10 else 2
                lt = (t - 8) % 2
                kcol = KOFF[sq_i] + lt * 128
                vt = VT0[sq_i] + lt
                ld(nk_d[(t - 8) * 128:(t - 7) * 128, :], kaf, [], ['kaf'], pool='o')
                ld(nv_d[(t - 8) * 128:(t - 7) * 128, :], ms[b][:, 384:512], [], [km], pool='o')
            tr(psb[7][:, 256:384], kab, ident, ['kafb', 'ident'], ['ps7'])
            cp('dve', kT_all[:, kcol:kcol + 128], psb[7][:, 256:384], ['ps7'], ['kT'])
            cp('pool', Vaug[:, vt, :, 0:64], ms[b][:, 384:512].rearrange('p (g d) -> p g d', d=64), [km], ['Vaug'])
        def run_pipeline(stages, n):
            for k in range(n + len(stages) - 1):
                lists = []
                for si_, f in enumerate(stages):
                    it = k - si_
                    if 0 <= it < n:
                        lists.append(P.record(lambda f=f, it=it: f(it)))
                P.replay_merged(lists)

        run_pipeline([p1_stageA, p1_stageA2, p1_stageB], NT)
        if stop_after <= 1:
            for name, (ap_, key) in dict(r_p=(r_p, 'r_p'), kkn_p=(kkn_p, 'kkn_p'), qT=(qT, 'qT'), kT=(kT_all, 'kT'),
                                         Vaug=(Vaug, 'Vaug'), txT=(txT, 'txT'), sgT=(sgT, 'sgT'), v_p=(v_p, 'v_p')).items():
                if name in dbg_d:
                    shp = list(ap_.shape)
                    n = int(np.prod(shp[1:]))
                    flat = ap_
                    if len(shp) == 3:
                        flat = ap_.rearrange('p a b -> p (a b)')
                    elif len(shp) == 4:
                        flat = ap_.rearrange('p a b c -> p (a b c)')
                    P.barrier()
                    dbgf = A.view(PERS_END, [128, n], F32)
                    cp('dve', dbgf, flat, [key], ['dbgf'])
                    dump(name, dbgf, ['dbgf'])
            return finish()
        P.barrier(exclude_groups=[outg])
        yat = A.view(YY_OFF, [128, NT, 512], BF16)
        yrw = A.view(YY_OFF + 12288, [128, NT, 512], BF16)
        s2 = Stack(A, PERS_END, YY_OFF)
        PT = [s2.alloc([128, 512], BF16) for _ in range(3)]
        rec = [s2.alloc([128, 4], F32) for _ in range(2)]
        items = []
        for si, (t0, ntile) in enumerate(SEQS):
            nchunk = 2 if ntile == 8 else 1
            ntc = ntile // nchunk
            for pos in range(8):
                for ch in range(nchunk):
                    for kt in range(NKT[si]):
                        items.append((si, pos, ch, kt, ntc, t0 + ch * ntc))

        def s_mm(i):
            si, pos, ch, kt, ntc, tile0 = items[i]
            hp, g = pos // 2, pos % 2
            b = i % 3
            gs = slice(g * 64, (g + 1) * 64)
            mm(ps[b][:, 0:ntc * 128], kT_all[gs, KOFF[si] + kt * 128:KOFF[si] + (kt + 1) * 128],
               qT[gs, hp, tile0 * 128:(tile0 + ntc) * 128], True, True, ['kT', 'qT'], ['ps%d' % b])

        m2 = s2.alloc([2, 4 * D], F32)
        wm2 = [s2.alloc([128, 8, 512], BF16) for _ in range(2)]
        bmod2 = s2.alloc([1, 4 * D], BF16)
        gpost_b2 = s2.alloc([128, 2, D], F32)

        def mod_rest():
            ld(bmod2, bmod_d[:, 2 * D:6 * D], ['bmod2'], eng='pool', max_dma_last_dim=2048)
            ld(gpost_b2[:, 0, :], nrm_d[2].partition_broadcast(128), ['gpostb0'])
            ld(gpost_b2[:, 1, :], nrm_d[3].partition_broadcast(128), ['gpostb1'])
            wmod_v2 = wmod_d.rearrange('(c p) n -> p c n', p=128)
            ld(wm2[0], wmod_v2[:, :, 4 * 512:5 * 512], ['wm20'], eng='pool', pool='w')
            for j in range(4, 12):
                bw = wm2[j % 2]
                kb = 'wm2%d' % (j % 2)
                if j + 1 < 12:
                    ld(wm2[(j + 1) % 2], wmod_v2[:, :, (j + 1) * 512:(j + 2) * 512], ['wm2%d' % ((j + 1) % 2)], eng='pool', pool='w')
                pk = 'ps%d' % (5 + j % 2)
                pt = ps[5 + j % 2]
                for c in range(8):
                    mm(pt[0:2, :], silucT[:, c, :], bw[:, c, :], c == 0, False, ['silucT', kb], [pk])
                mm(pt[0:2, :], ones_bf[0:1, 0:2], bmod2[0:1, (j - 4) * 512:(j - 3) * 512], False, True, ['ones', 'bmod2'], [pk])
                cp('dve', m2[:, (j - 4) * 512:(j - 3) * 512], pt[0:2, :], [pk], ['m2'])
            for ki, kofs in enumerate((1, 2)):
                for c in range(8):
                    i2 = (ki * 8 + c) * 2
                    tr(ps[7][:, i2:i2 + 2], m2[0:2, kofs * D + c * 128: kofs * D + (c + 1) * 128], identf[0:2, 0:2], ['m2', 'cst'], ['ps7'])
            cp('dve', modT[:, 2:4].rearrange('p a b c -> p (a b c)'), ps[7][:, 0:32], ['ps7'], ['modT'])
            stt(A2T, modT[:, 3], 1.0, gT[:, :, 1:2].to_broadcast([128, 8, 2]), ALU.add, ALU.mult, ['modT', 'gT'], ['A2T'])
            for (Gb, kofs, gi, key) in ((G1b, 0, 0, 'G1b'), (G2b, 3, 1, 'G2b')):
                for row in range(2):
                    for blk in range(2):
                        pk = 'ps%d' % (5 + blk)
                        mm(ps[5 + blk][:, :], sel[0:2, row, :], m2[0:2, kofs * D + blk * 512: kofs * D + (blk + 1) * 512], True, True, ['cst', 'm2'], [pk])
                        tt('dve', Gb[:, row, blk * 512:(blk + 1) * 512], ps[5 + blk][:, :], gpost_b2[:, gi, blk * 512:(blk + 1) * 512], ALU.mult, [pk, 'gpostb%d' % gi], [key])

        l_mod = P.record(mod_rest)
        P.capture = l_att = []
        grp = 0
        s_mm(0)
        for i, (si, pos, ch, kt, ntc, tile0) in enumerate(items):
            hp, g = pos // 2, pos % 2
            b = i % 3
            if i + 1 < len(items):
                s_mm(i + 1)
            act(PT[b][:, 0:ntc * 128], ps[b][:, 0:ntc * 128], AF.Exp, ['ps%d' % b], ['PT%d' % b], scale=0.125)
            ob = 3 + grp % 2
            for j in range(ntc):
                mm(ps[ob][:, j * 65:(j + 1) * 65], PT[b][:, j * 128:(j + 1) * 128], Vaug[:, VT0[si] + kt, g, 0:65],
                   kt == 0 and j == 0, kt == NKT[si] - 1 and j == ntc - 1, ['PT%d' % b, 'Vaug'], ['ps%d' % ob], skip=True)
            if kt == NKT[si] - 1:
                ov = ps[ob][:, 0:ntc * 65].rearrange('p (j c) -> p j c', c=65)
                rc = rec[grp % 2]
                P.op('dve', lambda e, o_=rc[:, 0:ntc].unsqueeze(2), i_=ov[:, :, 64:65]: e.reciprocal(out=o_, in_=i_), ['ps%d' % ob], ['rec%d' % (grp % 2)])
                tt('dve', yat[:, tile0:tile0 + ntc, pos * 64:(pos + 1) * 64], ov[:, :, 0:64],
                   rc[:, 0:ntc].unsqueeze(2).to_broadcast([128, ntc, 64]), ALU.mult, ['ps%d' % ob, 'rec%d' % (grp % 2)], ['yat'])
                grp += 1
        P.capture = None
        P.replay_merged([l_att, l_mod])
        if stop_after <= 2:
            P.barrier()
            dbgf = A.view(PERS_END, [128, NT * 512], F32)
            cp('dve', dbgf, yat.rearrange('p a b -> p (a b)'), ['yat'], ['dbgf'])
            dump('yat', dbgf, ['dbgf'])
            return finish()
        P.barrier(exclude_groups=[outg])
        s3 = Stack(A, PERS_END, YY_OFF)
        s3b = Stack(A, CONST_END + 4 * 12288 + 2 * 3072, PERS_END)
        sg = s3.alloc([128, 512], F32)
        al = s3.alloc([128, 512], BF16)
        e_ex = s3.alloc([128, 512], F32)
        e_ng = s3.alloc([128, 512], F32)
        e_rm = s3.alloc([128, 512], F32)
        at_ = s3.alloc([128, 512], BF16)
        rt_ = s3.alloc([128, 512], BF16)
        bt_ = s3.alloc([128, 512], BF16)
        kt_ = s3.alloc([128, 512], BF16)
        bb = s3.alloc([128, 512], BF16)
        kka = s3.alloc([128, 512], BF16)
        kd = s3.alloc([128, 512], F32)
        rrk = s3.alloc([128, 512], F32)
        bkT = s3.alloc([128, 2, 4, 128], BF16)
        Q0T = s3.alloc([128, 8, 128], BF16)
        Qb = [s3.alloc([128, 8, 128], BF16) for _ in range(2)]
        QTb = [s3.alloc([128, 8, 128], BF16) for _ in range(2)]
        Rb = [s3.alloc([128, 8, 128], BF16) for _ in range(2)]
        Xb = s3.alloc([128, 512], BF16)
        Ub = s3.alloc([128, 512], BF16)
        ysum = s3.alloc([128, 512], F32)
        sqy = s3.alloc([128, 512], F32)
        ST = s3.alloc([128, 4, 64], F32)
        STb = s3.alloc([128, 4, 64], BF16)
        Pc = s3.alloc([128, 4], F32)
        stl = s3.alloc([64, 8, 64], F32)
        bsb = s3.alloc([128, NT, 8], F32)
        st3 = s3.alloc([128, 64], F32)
        bsf3 = s3.alloc([128, 3, 8], F32)
        arT2 = [s3.alloc([128, 4, 2, 128], BF16), s3b.alloc([128, 4, 2, 128], BF16)]
        AB2 = [s3.alloc([128, 8, 2, 128], BF16), s3b.alloc([128, 8, 2, 128], BF16)]
        AK2 = [s3.alloc([128, 8, 2, 128], BF16), s3b.alloc([128, 8, 2, 128], BF16)]
        TT2 = [s3b.alloc([128, 8, 128], BF16) for _ in range(2)]
        bh3 = [s3.alloc([128, 512], BF16), s3.alloc([128, 512], BF16), s3b.alloc([128, 512], BF16)]
        kh3 = [s3.alloc([128, 512], BF16), s3.alloc([128, 512], BF16), s3b.alloc([128, 512], BF16)]
        ein3 = [s3.alloc([128, 512], F32), s3.alloc([128, 512], F32), s3b.alloc([128, 512], F32)]
        gsb2 = [s3b.alloc([128, 512], BF16), s3.alloc([128, 512], BF16)]
        sgh = s3.alloc([128, 512], BF16)
        sgl = s3.alloc([128, 512], BF16)
        flat3 = lambda ap: ap.rearrange('p a b -> p (a b)')
        flat4 = lambda ap: ap.rearrange('p a b c -> p (a b c)')
        hv = lambda ap: ap.rearrange('p (h d) -> p h d', d=64)

        def prepA(t, d, p3):
            tok = slice(t * 128, (t + 1) * 128)
            e_in_, bh_, kh_ = ein3[p3], bh3[p3], kh3[p3]
            ke, kb, kk_ = 'e_in%d' % p3, 'bh%d' % p3, 'kh%d' % p3
            mm(ps[6][:, :], txT[0:64, tok], lora_w[0:64, d, :], True, False, ['txT', 'lora'], ['ps6'])
            mm(ps[6][:, :], ones_bf[0:1, 0:128], w0a0[0:1, d * 512:(d + 1) * 512], False, True, ['ones', 'lora'], ['ps6'])
            yield
            act(sg, ps[6][:, :], AF.Tanh, ['ps6'], ['sg'], scale=0.5)
            lt_eng = 'pool' if d == 1 else 'dve'
            cp(lt_eng, sgh, sg, ['sg'], ['sgh'])
            yield
            mm(ps[6][:, :], txT[64:128, tok], lora_w[64:128, d, :], True, False, ['txT', 'lora'], ['ps6'])
            mm(ps[6][:, :], ones_bf[64:65, 0:128], w0a0[64:65, d * 512:(d + 1) * 512], False, True, ['ones', 'lora'], ['ps6'])
            tt('pool', kka, k_p[:, t, :], vecs_b[:, 1, :], ALU.mult, ['k_p', 'vecs'], ['kka'])
            tt(lt_eng, sgl, sg, sgh, ALU.subtract, ['sg', 'sgh'], ['sgl'])
            yield
            act(kd, ps[6][:, :], AF.Tanh, ['ps6'], ['kd'], scale=0.5)
            ts('dve', al, kd, 0.5, 0.5, ALU.mult, ALU.add, ['kd'], ['al'])
            tt('pool', rrk, r_p[:, t, :], vecs_b[:, 2, :], ALU.mult, ['r_p', 'vecs'], ['rrk'])
            yield
            mm(ps[6][:, :], tri[:, 3 * d + 0, :], sgh, True, False, ['tri', 'sgh'], ['ps6'])
            mm(ps[6][:, :], tri[:, 3 * d + 0, :], sgl, False, True, ['tri', 'sgl'], ['ps6'])
            tt('pool', bb, kkn_p[:, t, :], al, ALU.mult, ['kkn_p', 'al'], ['bb'])
            stt(kd, al, -1.0, kka, ALU.add, ALU.mult, ['al', 'kka'], ['kd'])
            yield
            act(e_in_, ps[6][:, :], AF.Exp, ['ps6', 'cst'], [ke], scale=0.5 * CDEC, bias=cbias[:, d, 0:1])
            act(e_ng, ps[6][:, :], AF.Exp, ['ps6', 'cst'], ['e_ng'], scale=-0.5 * CDEC, bias=cbias[:, d, 1:2])
            tt('pool', kd, kd, k_p[:, t, :], ALU.add, ['kd', 'k_p'], ['kd'])
            yield
            mm(ps[6][:, :], tri[:, 3 * d + 1, :], sgh, True, False, ['tri', 'sgh'], ['ps6'])
            mm(ps[6][:, :], tri[:, 3 * d + 1, :], sgl, False, True, ['tri', 'sgl'], ['ps6'])
            tt('pool', rt_, r_p[:, t, :], e_in_, ALU.mult, ['r_p', ke], ['rt_'])
            tt('pool', bt_, bb, e_ng, ALU.mult, ['bb', 'e_ng'], ['bt_'])
            tt(lt_eng, kt_, kd, e_ng, ALU.mult, ['kd', 'e_ng'], ['kt_'])
            yield
            act(e_ex, ps[6][:, :], AF.Exp, ['ps6', 'cst'], ['e_ex'], scale=0.5 * CDEC, bias=cbias[:, d, 2:3])
            tt('dve', rrk, rrk, kd, ALU.mult, ['rrk', 'kd'], ['rrk'])
            yield
            mm(ps[6][:, :], tri[:, 3 * d + 2, :], sgh, True, False, ['tri', 'sgh'], ['ps6'])
            mm(ps[6][:, :], tri[:, 3 * d + 2, :], sgl, False, True, ['tri', 'sgl'], ['ps6'])
            stt(at_, e_ex, -1.0, kkn_p[:, t, :], ALU.mult, ALU.mult, ['e_ex', 'kkn_p'], ['at_'])
            if d == 1:
                red(bsb[:, t, :], hv(rrk), ['rrk'], ['bsb'])
            else:
                red(bsf3[:, p3, :], hv(rrk), ['rrk'], ['bsf%d' % p3])
            yield
            act(e_rm, ps[6][:, :], AF.Exp, ['ps6', 'cst'], ['e_rm'], scale=0.5 * CDEC, bias=cbias[:, d, 3:4])
            tt('pool', bh_, bb, e_rm, ALU.mult, ['bb', 'e_rm'], [kb])
            tt('pool', kh_, kd, e_rm, ALU.mult, ['kd', 'e_rm'], [kk_])
            yield

        def prepB(t, d, p2, nxt):
            arT, AB, AK, TT = arT2[p2], AB2[p2], AK2[p2], TT2[p2]
            kar, kab_, kak = 'arT%d' % p2, 'AB%d' % p2, 'AK%d' % p2
            for hp in range(4):
                cs = slice(hp * 128, (hp + 1) * 128)
                b_ar = ps[hp // 2]
                o = (hp % 2) * 256
                mm(b_ar[:, o:o + 128], at_[:, cs], ident, True, True, ['at_', 'ident'], ['ps%d' % (hp // 2)])
                mm(b_ar[:, o + 128:o + 256], rt_[:, cs], ident, True, True, ['rt_', 'ident'], ['ps%d' % (hp // 2)])
                mm(ps[2][:, cs], bt_[:, cs], ident, True, True, ['bt_', 'ident'], ['ps2'])
                mm(ps[3][:, cs], kt_[:, cs], ident, True, True, ['kt_', 'ident'], ['ps3'])
            fa = flat4(arT)
            cp('act', fa[:, 0:512], ps[0][:, :], ['ps0'], [kar])
            cp('act', fa[:, 512:1024], ps[1][:, :], ['ps1'], [kar])
            cp('act', flat3(bkT[:, 0]), ps[2][:, :], ['ps2'], ['bkT'])
            cp('act', flat3(bkT[:, 1]), ps[3][:, :], ['ps3'], ['bkT'])
            if nxt is not None:
                next(nxt, None)
            for half in range(2):
                for i in range(4):
                    h = half * 4 + i
                    hp, e = h // 2, h % 2
                    es = slice(e * 64, (e + 1) * 64)
                    c2 = slice((i // 2) * 256, (i // 2) * 256 + 256)
                    mm(ps[e][:, c2], bkT[es, 0, hp, :], arT[es, hp], True, True, ['bkT', kar], ['ps%d' % e])
                    mm(ps[2 + e][:, c2], bkT[es, 1, hp, :], arT[es, hp], True, True, ['bkT', kar], ['ps%d' % (2 + e)])
                    mm(ps[4 + e][:, (i // 2) * 128:(i // 2 + 1) * 128], arT[es, hp, 0, :], bkT[es, 0, hp, :], True, True, ['bkT', kar], ['ps%d' % (4 + e)])
                m2 = mmask[:, d, :].rearrange('p (a x) -> p a x', x=256)
                q2 = qtmask[:, d, 0:256].rearrange('p (a x) -> p a x', x=128)
                for e in range(2):
                    h0 = half * 4 + e
                    tt('dve', AB[:, h0:h0 + 3:2].rearrange('p a b c -> p a (b c)'), ps[e][:, :].rearrange('p (a x) -> p a x', x=256), m2,
                       ALU.mult, ['ps%d' % e, 'mmask'], [kab_ + '.%d' % half])
                    tt('dve', Q0T[:, h0:h0 + 3:2], ps[4 + e][:, 0:256].rearrange('p (a x) -> p a x', x=128), q2,
                       ALU.mult, ['ps%d' % (4 + e), 'qtmask'], ['Q0T.%d' % half])
                for e in range(2):
                    h0 = half * 4 + e
                    tt('dve', AK[:, h0:h0 + 3:2].rearrange('p a b c -> p a (b c)'), ps[2 + e][:, :].rearrange('p (a x) -> p a x', x=256), m2,
                       ALU.mult, ['ps%d' % (2 + e), 'mmask'], [kak])
                if nxt is not None:
                    next(nxt, None)
            for j in range(2):
                tt('pool', Rb[0][:, j * 4:j * 4 + 4], AB[:, j * 4:j * 4 + 4, 0, :], ident.unsqueeze(1).to_broadcast([128, 4, 128]), ALU.add,
                   [kab_ + '.%d' % j, 'ident'], ['R0.%d' % j])
            Qp, QTp = AB[:, :, 0, :], Q0T
            kq = [kab_ + '.0', kab_ + '.1']
            kqt = ['Q0T.0', 'Q0T.1']

            def r_level(lev, QTl, kqtl):
                Rp = Rb[(lev - 1) % 2]
                Rn = TT if lev == 6 else Rb[lev % 2]
                kn = (lambda j: 'TT%d.%d' % (p2, j)) if lev == 6 else (lambda j: 'R%d.%d' % (lev % 2, j))
                for h in range(8):
                    j = h // 4
                    mm(ps[4 + j][:, (h % 4) * 128:(h % 4 + 1) * 128], QTl[:, h, :], Rp[:, h, :], True, True,
                       [kqtl[j], 'R%d.%d' % ((lev - 1) % 2, j)], ['ps%d' % (4 + j)])
                for j in range(2):
                    tt('dve', flat3(Rn[:, j * 4:j * 4 + 4]), ps[4 + j][:, :], flat3(Rp[:, j * 4:j * 4 + 4]), ALU.add,
                       ['ps%d' % (4 + j), 'R%d.%d' % ((lev - 1) % 2, j)], [kn(j)])

            pend = None
            for lev in range(1, 7):
                pi = lev % 2
                nkq = ['Q%d.%d' % (pi, j) for j in range(2)]
                nkqt = ['QT%d.%d' % (pi, j) for j in range(2)]
                for j in range(2):
                    if lev <= 5:
                        for h in range(j * 4, j * 4 + 4):
                            mm(ps[j][:, (h % 4) * 128:(h % 4 + 1) * 128], QTp[:, h, :], Qp[:, h, :], True, True, [kq[j], kqt[j]], ['ps%d' % j])
                    for h in range(j * 4, j * 4 + 4):
                        mm(ps[2 + j][:, (h % 4) * 128:(h % 4 + 1) * 128], Qp[:, h, :], QTp[:, h, :], True, True, [kq[j], kqt[j]], ['ps%d' % (2 + j)])
                    if lev <= 5:
                        cp('act', flat3(Qb[pi][:, j * 4:j * 4 + 4]), ps[j][:, :], ['ps%d' % j], [nkq[j]])
                    cp('act', flat3(QTb[pi][:, j * 4:j * 4 + 4]), ps[2 + j][:, :], ['ps%d' % (2 + j)], [nkqt[j]])
                if pend is not None:
                    r_level(*pend)
                Qp, QTp, kq, kqt = Qb[pi], QTb[pi], nkq, nkqt
                pend = (lev, QTp, kqt)
                if nxt is not None:
                    next(nxt, None)
            r_level(*pend)
            if nxt is not None:
                for _ in nxt:
                    pass

        def chain(stp, p2, p3):
            t, d, si = stp['t'], stp['d'], stp['si']
            arT, AB, AK, TT = arT2[p2], AB2[p2], AK2[p2], TT2[p2]
            kar, kab_, kak = 'arT%d' % p2, 'AB%d' % p2, 'AK%d' % p2
            ktt = ['TT%d.0' % p2, 'TT%d.1' % p2]
            e_in_, bh_, kh_ = ein3[p3], bh3[p3], kh3[p3]
            ke, kb, kk_ = 'e_in%d' % p3, 'bh%d' % p3, 'kh%d' % p3
            if stp['first']:
                if si == 0:
                    ld(stl, st_d[d].rearrange('h v k -> v h k'), ['stl'])
                    for hp in range(4):
                        tr(ps[7][:, hp * 64:(hp + 1) * 64], flat3(stl[:, 2 * hp:2 * hp + 2, :]), identf[0:64, 0:64], ['stl', 'cst'], ['ps7'])
                    cp('dve', flat3(ST), ps[7][:, 0:256], ['ps7'], ['ST'])
                    cp('act', flat3(STb), ps[7][:, 0:256], ['ps7'], ['STb'])
                else:
                    P.op('dve', lambda e: e.memset(flat3(ST), 0.0), [], ['ST'])
                    P.op('pool', lambda e: e.memset(flat3(STb), 0.0), [], ['STb'])
            for hp in range(4):
                mm(ps[7][:, hp:hp + 1], e_in_[:, hp * 128:(hp + 1) * 128], onehot[:, d:d + 1], True, True, [ke, 'cst'], ['ps7'])
            cp('dve', Pc, ps[7][:, 0:4], ['ps7'], ['Pc'])
            if d == 0:
                tok = slice(t * 128, (t + 1) * 128)
                mm(ps[7][:, :], sgT[:, tok], gup_bf, True, True, ['sgT', 'lora'], ['ps7'])
                cp('act', gsb2[p2], ps[7][:, :], ['ps7'], ['gsb%d' % p2])
            for h in range(8):
                hp, e = h // 2, h % 2
                es = slice(e * 64, (e + 1) * 64)
                hc = slice(h * 64, (h + 1) * 64)
                mm(ps[7][:, hc], arT[es, hp, 0, :], STb[es, hp, :], True, False, [kar, 'STb'], ['ps7'])
                mm(ps[7][:, hc], AK[:, h, 0, :], v_p[:, t, hc], False, True, [kak, 'v_p'], ['ps7'])
            cp('act', Xb, ps[7][:, :], ['ps7'], ['Xb'])
            P.spacer(10)
            for h in range(8):
                hc = slice(h * 64, (h + 1) * 64)
                mm(ps[7][:, hc], TT[:, h, :], Xb[:, hc], True, True, ktt + ['Xb'], ['ps7'])
            cp('act', Ub, ps[7][:, :], ['ps7'], ['Ub'])
            P.spacer(10)
            for h in range(8):
                hp, e = h // 2, h % 2
                es = slice(e * 64, (e + 1) * 64)
                hc = slice(h * 64, (h + 1) * 64)
                mm(ps[7][:, hc], arT[es, hp, 1, :], STb[es, hp, :], True, False, [kar, 'STb'], ['ps7'])
                mm(ps[7][:, hc], AB[:, h, 1, :], Ub[:, hc], False, False, [kab_ + '.0', kab_ + '.1', 'Ub'], ['ps7'])
                mm(ps[7][:, hc], AK[:, h, 1, :], v_p[:, t, hc], False, True, [kak, 'v_p'], ['ps7'])
            if d == 1:
                cp('act', yrw[:, t, :], ps[7][:, :], ['ps7'], ['yrw'])
            else:
                tt('dve', ysum, ps[7][:, :], yrw[:, t, :], ALU.add, ['ps7', 'yrw'], ['ysum'])
            P.spacer(8)
            for hp in range(4):
                pc = slice(hp * 128, (hp + 1) * 128)
                mm(ps[7][:, pc], bh_[:, pc], Ub[:, pc], True, False, [kb, 'Ub'], ['ps7'])
                mm(ps[7][:, pc], kh_[:, pc], v_p[:, t, pc], False, True, [kk_, 'v_p'], ['ps7'])
            tt('dve', ST, ST, Pc.unsqueeze(2).to_broadcast([128, 4, 64]), ALU.mult, ['ST', 'Pc'], ['ST'])
            pv = ps[7][:, :].rearrange('p (a x) -> p a x', x=128)
            tt('dve', ST[0:64], ST[0:64], pv[0:64, :, 0:64], ALU.add, ['ST', 'ps7'], ['ST'])
            tt('dve', ST[64:128], ST[64:128], pv[64:128, :, 64:128], ALU.add, ['ST', 'ps7'], ['ST'])
            cp('pool', STb, ST, ['ST'], ['STb'])
            if stp['last'] and si > 0:
                for hp in range(4):
                    tr(ps[7][0:64, hp * 128:(hp + 1) * 128], ST[:, hp, :], identf, ['ST', 'cst'], ['ps7'])
                cp('dve', flat3(stl), ps[7][0:64, :], ['ps7'], ['stl'])
                ld(ns_d[si - 1, d].rearrange('h v k -> v h k'), stl, [], ['stl'])
            if d == 1:
                return
            red(st3[:, 8:16], hv(ysum), ['ysum'], ['gn.s1'])
            tt('pool', sqy, ysum, ysum, ALU.mult, ['ysum'], ['sqy'])
            red(st3[:, 16:24], hv(sqy), ['sqy'], ['gn.s2'])
            ts('dve', st3[:, 24:32], st3[:, 8:16], 1.0 / 64, None, ALU.mult, None, ['gn.s1'], ['gn.mean'])
            tt('dve', st3[:, 32:40], st3[:, 24:32], st3[:, 24:32], ALU.mult, ['gn.mean'], ['gn.msq'])
            stt(st3[:, 40:48], st3[:, 16:24], 1.0 / 64, st3[:, 32:40], ALU.mult, ALU.subtract, ['gn.s2', 'gn.msq'], ['gn.ss'])
            rstd_of(st3[:, 40:48], st3[:, 48:56], st3[:, 56:64], 1.0, 64e-5, 'gn')
            b8 = lambda ap: ap.unsqueeze(2).to_broadcast([128, 8, 64])
            tt('pool', hv(ysum), hv(ysum), b8(st3[:, 24:32]), ALU.subtract, ['ysum', 'gn.mean'], ['ysum'])
            tt('pool', hv(ysum), hv(ysum), b8(st3[:, 56:64]), ALU.mult, ['ysum', 'gn.rstd'], ['ysum'])
            tt('pool', ysum, ysum, vecs_b[:, 3, :], ALU.mult, ['ysum', 'vecs'], ['ysum'])
            tt('pool', ysum, ysum, vecs_b[:, 4, :], ALU.add, ['ysum', 'vecs'], ['ysum'])
            tt('dve', bsf3[:, p3, :], bsf3[:, p3, :], bsb[:, t, :], ALU.add, ['bsf%d' % p3, 'bsb'], ['bsf%d' % p3])
            tt('pool', hv(sqy), hv(v_p[:, t, :]), b8(bsf3[:, p3, :]), ALU.mult, ['v_p', 'bsf%d' % p3, 'sqy'], ['sqy'])
            tt('pool', ysum, ysum, sqy, ALU.add, ['ysum', 'sqy'], ['ysum'])
            tt('pool', yrw[:, t, :], ysum, gsb2[p2], ALU.mult, ['ysum', 'gsb%d' % p2], ['yrw'])

        steps = []
        for d in (1, 0):
            for si, (t0, ntile) in enumerate(SEQS):
                tiles = list(range(t0 + ntile - 1, t0 - 1, -1)) if d == 1 else list(range(t0, t0 + ntile))
                for k, t in enumerate(tiles):
                    steps.append(dict(t=t, d=d, si=si, first=(k == 0), last=(k == len(tiles) - 1)))
        NS = len(steps)
        gA = lambda i: prepA(steps[i]['t'], steps[i]['d'], i % 3) if i < NS else None
        for _ in gA(0):
            pass
        prepB(steps[0]['t'], steps[0]['d'], 0, gA(1))
        for i, stp in enumerate(steps):
            lc = P.record(lambda: chain(stp, i % 2, i % 3))
            ly = P.record(lambda: prepB(steps[i + 1]['t'], steps[i + 1]['d'], (i + 1) % 2, gA(i + 2))) if i + 1 < NS else []
            P.replay_merged([lc, ly])
        if stop_after <= 3:
            P.barrier()
            dbgf = A.view(PERS_END, [128, NT * 512], F32)
            cp('dve', dbgf, flat3(yrw), ['yrw'], ['dbgf'])
            dump('yrw', dbgf, ['dbgf'])
            return finish()

        P.barrier(exclude_groups=[outg])
        K1 = 1024
        delta1 = A.view(CONST_END, [128, NT, D], BF16)
        h2T = A.view(132 * K1, [128, 8, NTOK], BF16)
        s4 = Stack(A, 64 * K1, 130 * K1)
        Wout = s4.alloc([128, 8, D], BF16)
        mixT = [s4.alloc([128, 8, 128], BF16) for _ in range(2)]
        xt2 = [s4.alloc([128, D], F32) for _ in range(2)]
        x1 = s4.alloc([128, D], F32)
        junk4 = s4.alloc([128, D], F32)
        junk4c = s4.alloc([128, D], BF16)
        xs2 = [s4.alloc([128, D], BF16) for _ in range(2)]
        st4 = s4.alloc([128, 16], F32)
        g_wo = P.dma_group()
        ld(Wout[:, 0:4, :], wout_d[0:512, :].rearrange('(c p) n -> p c n', p=128), ['Wout'], group=g_wo, eng='pool')
        ld(Wout[0:64, 4:8, :], wout_d[512:768, :].rearrange('(c p) n -> p c n', p=64), ['Wout'], group=g_wo, eng='pool')
        ld(Wout[64:128, 4:8, :], wout_d[768:1024, :].rearrange('(c p) n -> p c n', p=64), ['Wout'], group=g_wo, eng='pool')
        def p4_stageA(t):
            b = t % 2
            row = 0 if t < 8 else 1
            for c in range(8):
                src = yrw[:, t, c * 128:(c + 1) * 128] if c < 4 else yat[:, t, (c - 4) * 128:(c - 3) * 128]
                tr(psb[0][:, c * 128:(c + 1) * 128], src, ident, ['yrw', 'yat', 'ident'], ['ps0'])
            cp('act', mixT[b].rearrange('p a b -> p (a b)'), psb[0][:, 0:1024], ['ps0'], ['mixT%d' % b])
            for blk in range(2):
                for c in range(8):
                    mm(ps[1 + blk][:, :], mixT[b][:, c, :], Wout[:, c, blk * 512:(blk + 1) * 512], c == 0, c == 7, ['mixT%d' % b, 'Wout'], ['ps%d' % (1 + blk)])
            act(junk4[:, 0:512], ps[1][:, :], AF.Square, ['ps1'], ['junk4a'], accum=st4[:, 0:1])
            act(junk4[:, 512:1024], ps[2][:, :], AF.Square, ['ps2'], ['junk4b'], accum=st4[:, 1:2])
            tt('dve', st4[:, 2:3], st4[:, 0:1], st4[:, 1:2], ALU.add, ['junk4a', 'junk4b'], ['m4.ss'])
            rstd_of(st4[:, 2:3], st4[:, 3:4], st4[:, 4:5], 1.0 / D, 1e-6, 'm4', use_pow=False)
            for blk in range(2):
                stt(delta1[:, t, blk * 512:(blk + 1) * 512], ps[1 + blk][:, :], st4[:, 4:5], G1b[:, row, blk * 512:(blk + 1) * 512],
                    ALU.mult, ALU.mult, ['ps%d' % (1 + blk), 'm4.rstd', 'G1b'], ['delta1'])
        def p4_stageB(t):
            b = t % 2
            row = 0 if t < 8 else 1
            ld(xt2[b], xs_d[t * 128:(t + 1) * 128, :], ['xt2%d' % b], pool='x')
            tt('dve', x1, xt2[b], delta1[:, t, :], ALU.add, ['xt2%d' % b, 'delta1'], ['x1'])
            act(junk4c, x1, AF.Square, ['x1'], ['junk4c', 'n2.ss'], accum=st4[:, 8:9])
            rstd_of(st4[:, 8:9], st4[:, 9:10], st4[:, 10:11], 1.0 / D, 1e-6, 'n2', use_pow=False)
            ts('dve', xs2[b], x1, st4[:, 10:11], None, ALU.mult, None, ['x1', 'n2.rstd'], ['xs2%d' % b])
        def p4_stageB2(t):
            b = t % 2
            row = 0 if t < 8 else 1
            for c in range(8):
                tr(psb[3][:, c * 128:(c + 1) * 128], xs2[b][:, c * 128:(c + 1) * 128], ident, ['xs2%d' % b, 'ident'], ['ps3'])
            for c in range(8):
                act(h2T[:, c, t * 128:(t + 1) * 128], psb[3][:, c * 128:(c + 1) * 128], AF.Identity, ['ps3', 'A2T', 'modT'], ['h2T'],
                    scale=A2T[:, c, row:row + 1], bias=B2T[:, c, row:row + 1])
        run_pipeline([p4_stageA, p4_stageB, p4_stageB2], NT)
        if stop_after <= 4:
            P.barrier()
            dbgf = A.view(64 * K1, [128, 8 * NTOK], F32)
            cp('dve', dbgf, h2T.rearrange('p a b -> p (a b)'), ['h2T'], ['dbgf'])
            dump('h2T', dbgf, ['dbgf'])
            return finish()

        P.barrier(exclude_groups=[outg])
        NPAD = 1542
        PADOFF = [1, 1027, 1285]
        ptok = lambda t: (1 + t * 128) if t < 8 else (1027 + (t - 8) * 128 if t < 10 else 1285 + (t - 10) * 128)
        actT = A.view(64 * K1, [128, 22, NPAD], BF16)
        s5 = Stack(A, 156 * K1, ARENA_BYTES)
        Wup = [s5.alloc([128, 8, 512], BF16) for _ in range(2)]
        ug = s5.alloc([128, NPAD], F32)
        uv = s5.alloc([128, NPAD], F32)
        cg = s5.alloc([128, NPAD], F32)
        cvv = s5.alloc([128, NPAD], F32)
        sgt = s5.alloc([128, NPAD], F32)
        fup_v = fup_d.rearrange('(c p) n -> p c n', p=128)
        P.op('pool', lambda e: e.memset(ug, 0.0), [], ['ug'])
        P.op('pool', lambda e: e.memset(uv, 0.0), [], ['uv'])

        def conv_chunk(u, dst, banks, jc, ku, kc):
            w0, w1, w2, bb_ = (convT[:, jc, i:i + 1] for i in range(4))
            cp('act', u[:, 1:513], ps[banks[0]][:, :], ['ps%d' % banks[0]], [ku])
            cp('act', u[:, 513:1025], ps[banks[1]][:, :], ['ps%d' % banks[1]], [ku])
            cp('act', u[:, 1027:1283], ps[banks[2]][:, 0:256], ['ps%d' % banks[2]], [ku])
            cp('act', u[:, 1285:1541], ps[banks[2]][:, 256:512], ['ps%d' % banks[2]], [ku])
            ts('pool', dst[:, 1:1541], u[:, 1:1541], w1, bb_, ALU.mult, ALU.add, [ku, 'convT'], [kc])
            stt(dst[:, 1:1541], u[:, 0:1540], w0, dst[:, 1:1541], ALU.mult, ALU.add, [ku, kc, 'convT'], [kc])
            stt(dst[:, 1:1541], u[:, 2:1542], w2, dst[:, 1:1541], ALU.mult, ALU.add, [ku, kc, 'convT'], [kc])

        def ld_piece(pi):
            g_u = P.dma_group('w')
            kw = 'Wup%d' % (pi % 2)
            ld(Wup[pi % 2][:, :, 0:256], fup_v[:, :, pi * 256:(pi + 1) * 256], [kw], group=g_u, eng='pool')
            ld(Wup[pi % 2][:, :, 256:512], fup_v[:, :, DFF + pi * 256:DFF + (pi + 1) * 256], [kw], group=g_u, eng='pool')

        ld_piece(0)
        for pi in range(11):
            wb = Wup[pi % 2]
            kw = 'Wup%d' % (pi % 2)
            for pp in range(2):
                j = pi * 2 + pp
                for half, (col0, banks) in enumerate(((pp * 128, (0, 1, 2)), (256 + pp * 128, (3, 4, 5)))):
                    for tg in range(3):
                        for c in range(8):
                            mm(ps[banks[tg]][:, :], wb[:, c, col0:col0 + 128], h2T[:, c, tg * 512:(tg + 1) * 512], c == 0, c == 7, [kw, 'h2T'], ['ps%d' % banks[tg]])
                if pp == 0 and pi + 1 < 11:
                    ld_piece(pi + 1)
                conv_chunk(ug, cg, (0, 1, 2), j, 'ug', 'cg')
                conv_chunk(uv, cvv, (3, 4, 5), 22 + j, 'uv', 'cvv')
                act(sgt[:, 1:1541], cg[:, 1:1541], AF.Silu, ['cg'], ['sgt'])
                tt('dve', actT[:, j, 1:1541], sgt[:, 1:1541], cvv[:, 1:1541], ALU.mult, ['sgt', 'cvv'], ['actT'])
        if stop_after <= 5:
            return finish()

        P.barrier(exclude_groups=[outg])
        Wdn = A.view(132 * K1, [128, 22, D], BF16)
        s6 = Stack(A, 176 * K1, ARENA_BYTES)
        xt3 = [s6.alloc([128, D], F32) for _ in range(2)]
        d2 = s6.alloc([128, D], F32)
        junk6 = s6.alloc([128, D], F32)
        yo = [s6.alloc([128, D], F32) for _ in range(2)]
        st6 = s6.alloc([128, 16], F32)
        g_wd = P.dma_group()
        fdn_v = fdn_d.rearrange('(j p) n -> p j n', p=128)
        WDK = []
        for qi, (j0, j1) in enumerate(((0, 6), (6, 11), (11, 17), (17, 22))):
            WDK.append((j0, j1, 'Wdn.%d' % qi))
            ld(Wdn[:, j0:j1, :], fdn_v[:, j0:j1, :], ['Wdn.%d' % qi], eng='pool')
        wdkey = lambda j: [k for (j0, j1, k) in WDK if j0 <= j < j1]
        for t in range(NT):
            b = t % 2
            row = 0 if t < 8 else 1
            pb = 2 * b
            for blk in range(2):
                for j in range(22):
                    mm(ps[pb + blk][:, :], actT[:, j, ptok(t):ptok(t) + 128], Wdn[:, j, blk * 512:(blk + 1) * 512], j == 0, j == 21, ['actT'] + wdkey(j), ['ps%d' % (pb + blk)])
            act(junk6[:, 0:512], ps[pb][:, :], AF.Square, ['ps%d' % pb], ['junk6a'], accum=st6[:, 0:1])
            act(junk6[:, 512:1024], ps[pb + 1][:, :], AF.Square, ['ps%d' % (pb + 1)], ['junk6b'], accum=st6[:, 1:2])
            tt('dve', st6[:, 2:3], st6[:, 0:1], st6[:, 1:2], ALU.add, ['junk6a', 'junk6b'], ['f6.ss'])
            rstd_of(st6[:, 2:3], st6[:, 3:4], st6[:, 4:5], 1.0 / D, 1e-6, 'f6', use_pow=False)
            for blk in range(2):
                stt(d2[:, blk * 512:(blk + 1) * 512], ps[pb + blk][:, :], st6[:, 4:5], G2b[:, row, blk * 512:(blk + 1) * 512],
                    ALU.mult, ALU.mult, ['ps%d' % (pb + blk), 'f6.rstd', 'G2b'], ['d2'])
            ld(xt3[b], xs_d[t * 128:(t + 1) * 128, :], ['xt3%d' % b], pool='x')
            tt('dve', yo[b], xt3[b], delta1[:, t, :], ALU.add, ['xt3%d' % b, 'delta1'], ['yo%d' % b])
            tt('pool', yo[b], yo[b], d2, ALU.add, ['yo%d' % b, 'd2'], ['yo%d' % b])
            ld(y_d[t * 128:(t + 1) * 128, :], yo[b], [], ['yo%d' % b], pool='o')
        P.barrier()
        P.emit(final_groups=[outg])
        return nc
        return finish()


def shard_inputs(inp, core):
    f = lambda a: np.ascontiguousarray(np.asarray(a, dtype=np.float32))
    m = {}
    m['xs'] = f(np.concatenate([inp['x_sample'][core], inp['x_prompt'][2 * core], inp['x_prompt'][2 * core + 1]], 0))
    m['cvec'] = f(np.stack([inp['c'][core], inp['c_ctx']], 0))
    m['ck'] = f(inp['cache_k'][core, 0].reshape(256, 128))
    m['cv'] = f(inp['cache_v'][core, 0].reshape(256, 128))
    m['st'] = f(inp['state_rwkv'][core, 0])
    m['w_mod'] = f(inp['w_mod'][0])
    m['b_mod'] = f(inp['b_mod'][0].reshape(1, -1))
    m['norms'] = f(np.stack([inp['norm_mix_pre'][0], inp['norm_ffn_pre'][0], inp['norm_mix_post'][0], inp['norm_ffn_post'][0]], 0))
    m['w_in'] = f(inp['w_in'][0])
    m['w0'] = f(inp['w0'][0].reshape(1, -1))
    m['w_up'] = f(inp['w_up'][0])
    m['a0'] = f(inp['a0'][0].reshape(1, -1))
    m['a_up'] = f(inp['a_up'][0])
    m['g_up'] = f(inp['g_up'][0])
    m['vecs'] = f(np.stack([inp['k_k'][0], inp['k_a'][0], inp['r_k'][0].reshape(-1), inp['gn_w'][0], inp['gn_b'][0]], 0))
    m['qkn'] = f(np.stack([inp['q_norm'][0], inp['k_norm'][0]], 0))
    m['w_out'] = f(inp['w_out'][0])
    m['ffn_up'] = f(inp['ffn_up'][0])
    m['convc'] = f(np.concatenate([inp['conv_w'][0], inp['conv_b'][0][None]], 0))
    m['ffn_down'] = f(inp['ffn_down'][0])
    m['cst'] = pack_consts()
    return m


def kernel(**inp):
    inp = {k: np.asarray(v) for k, v in inp.items()}
    nc = build()
    in_maps = [shard_inputs(inp, c) for c in range(8)]
    res = run_bass_kernel_spmd(nc, in_maps, core_ids=list(range(8)))
    R = res.results
    y_prompt = np.zeros((16, 256, 1024), np.float32)
    y_sample = np.zeros((8, 1024, 1024), np.float32)
    nk = np.zeros((16, 1, 256, 2, 64), np.float32)
    nv = np.zeros((16, 1, 256, 2, 64), np.float32)
    ns = np.zeros((16, 1, 2, 8, 64, 64), np.float32)
    for c in range(8):
        y = R[c]['y']
        y_sample[c] = y[0:1024]
        y_prompt[2 * c] = y[1024:1280]
        y_prompt[2 * c + 1] = y[1280:1536]
        nk[2 * c, 0] = R[c]['nk'][0:256].reshape(256, 2, 64)
        nk[2 * c + 1, 0] = R[c]['nk'][256:512].reshape(256, 2, 64)
        nv[2 * c, 0] = R[c]['nv'][0:256].reshape(256, 2, 64)
        nv[2 * c + 1, 0] = R[c]['nv'][256:512].reshape(256, 2, 64)
        ns[2 * c, 0] = R[c]['ns'][0]
        ns[2 * c + 1, 0] = R[c]['ns'][1]
    return (y_prompt, y_sample, nk, nv, ns)
```
